# Optimizing a Trainium2 kernel written in Bass

```python
import jax, jax.numpy as jnp
from jax import lax
import numpy as np

D_MODEL = 1024
BATCH = 16
SEQ = 2048
DEPTH = 1

D_MIX = D_MODEL
ML_HEADS = 4
ML_DH = 128
ML_WIDTH = ML_HEADS * ML_DH
ML_CHUNK = 64
CONV_K = 4
NSA_HEADS = 8
NSA_KV_HEADS = 2
NSA_DH = 64
NSA_HPG = NSA_HEADS // NSA_KV_HEADS
NSA_WIDTH = NSA_HEADS * NSA_DH
NSA_KV_WIDTH = NSA_KV_HEADS * NSA_DH
CMP_BLOCK = 32
CMP_STRIDE = 16
CMP_HIDDEN = 256
SLC_BLOCK = 64
SLC_TOPN = 16
WINDOW = 512
Q_BLOCK = 128
N_BRANCH = 3
D_IN = 4 * ML_WIDTH + 2 * ML_HEADS + NSA_WIDTH + 6 * NSA_KV_WIDTH + NSA_HEADS * N_BRANCH
D_FF = 2816
EPS = 1e-6
NEG = -1e30
FORCE = 1e30

kernel_name = 'hybrid_mlstm_nsa_macaron'


def rmsnorm(x, g):
    xf = x.astype(jnp.float32)
    y = xf * lax.rsqrt(jnp.mean(xf * xf, axis=-1, keepdims=True) + EPS)
    return (y * g.astype(jnp.float32)).astype(x.dtype)


def swiglu(x, w1, w3, w2):
    return (jax.nn.silu(x @ w1) * (x @ w3)) @ w2


def causal_conv(x, w):
    k, c = w.shape
    return lax.conv_general_dilated(x, w[:, None, :].astype(x.dtype), window_strides=(1,),
                                    padding=[(k - 1, 0)],
                                    dimension_numbers=('NWC', 'WIO', 'NWC'),
                                    feature_group_count=c)


def head_rmsnorm(h, g):
    H, D = h.shape[-2:]
    y = h * lax.rsqrt(jnp.mean(h * h, axis=-1, keepdims=True) + EPS)
    return y * g.reshape(H, D).astype(jnp.float32)


def mlstm_chunkwise(q, k, v, i_pre, f_pre):
    B, T, H, D = q.shape
    L = ML_CHUNK
    nc = T // L
    f32 = jnp.float32

    def to_chunks(a):
        return a.astype(f32).reshape(B, nc, L, H, D).transpose(0, 3, 1, 2, 4)

    q = to_chunks(q) * D ** -0.5
    k = to_chunks(k)
    v = to_chunks(v)
    log_i = i_pre.astype(f32).reshape(B, nc, L, H).transpose(0, 3, 1, 2)
    log_f = jax.nn.log_sigmoid(f_pre.astype(f32)).reshape(B, nc, L, H).transpose(0, 3, 1, 2)
    b = jnp.cumsum(log_f, axis=-1)
    g = b[..., -1]
    causal = jnp.tril(jnp.ones((L, L), dtype=bool))
    d_log = jnp.where(causal, b[..., :, None] - b[..., None, :] + log_i[..., None, :], -jnp.inf)

    w_end = g[..., None] - b + log_i
    m_loc = jnp.max(w_end, axis=-1)
    e = jnp.exp(w_end - m_loc[..., None])
    c_loc = jnp.einsum('bhcl,bhcld,bhcle->bhcde', e, v, k)
    n_loc = jnp.einsum('bhcl,bhcle->bhce', e, k)

    def step(carry, xs):
        c, n, m = carry
        cl, nl, ml, gc = xs
        m_new = jnp.maximum(gc + m, ml)
        a = jnp.exp(gc + m - m_new)
        s = jnp.exp(ml - m_new)
        c_new = a[..., None, None] * c + s[..., None, None] * cl
        n_new = a[..., None] * n + s[..., None] * nl
        return (c_new, n_new, m_new), (c, n, m)

    init = (jnp.zeros((B, H, D, D), f32), jnp.zeros((B, H, D), f32), jnp.zeros((B, H), f32))
    xs = (c_loc.transpose(2, 0, 1, 3, 4), n_loc.transpose(2, 0, 1, 3),
          m_loc.transpose(2, 0, 1), g.transpose(2, 0, 1))
    _, (c_prev, n_prev, m_prev) = lax.scan(step, init, xs)
    c_prev = c_prev.transpose(1, 2, 0, 3, 4)
    n_prev = n_prev.transpose(1, 2, 0, 3)
    m_prev = m_prev.transpose(1, 2, 0)

    m_inter = b + m_prev[..., None]
    m_t = jnp.maximum(m_inter, jnp.max(d_log, axis=-1))
    s = jnp.einsum('bhcld,bhcsd->bhcls', q, k) * jnp.exp(d_log - m_t[..., None])
    r = jnp.exp(m_inter - m_t)
    num = jnp.einsum('bhcls,bhcsd->bhcld', s, v) + r[..., None] * jnp.einsum('bhcde,bhcle->bhcld', c_prev, q)
    den = jnp.sum(s, axis=-1) + r * jnp.einsum('bhce,bhcle->bhcl', n_prev, q)
    h = num / jnp.maximum(jnp.abs(den), jnp.exp(-m_t))[..., None]
    return h.transpose(0, 2, 3, 1, 4).reshape(B, T, H, D)


def compress_blocks(x, pe, w1, b1, w2):
    B, T, G, Dh = x.shape
    n_cmp = (T - CMP_BLOCK) // CMP_STRIDE + 1
    idx = jnp.arange(n_cmp)[:, None] * CMP_STRIDE + jnp.arange(CMP_BLOCK)[None, :]
    blk = x[:, idx] + pe[:, None, :]
    flat = blk.transpose(0, 1, 3, 2, 4).reshape(B, n_cmp, G, CMP_BLOCK * Dh)
    return jax.nn.silu(flat @ w1 + b1) @ w2


def nsa_compressed(q, kc, vc):
    T = q.shape[1]
    n_cmp = kc.shape[1]
    s = jnp.einsum('btghd,bngd->bghtn', q, kc.astype(jnp.float32))
    cmp_end = jnp.arange(n_cmp) * CMP_STRIDE + CMP_BLOCK - 1
    valid = cmp_end[None, :] <= jnp.arange(T)[:, None]
    p = jax.nn.softmax(jnp.where(valid, s, NEG), axis=-1)
    p = jnp.where(valid, p, 0.0)
    o = jnp.einsum('bghtn,bngd->btghd', p, vc.astype(jnp.float32))
    return o, p


def select_blocks(p_cmp, T):
    n_cmp = p_cmp.shape[-1]
    nslc = T // SLC_BLOCK
    c0 = jnp.arange(n_cmp)[:, None] * CMP_STRIDE
    s0 = jnp.arange(nslc)[None, :] * SLC_BLOCK
    overlap = jnp.clip(jnp.minimum(c0 + CMP_BLOCK, s0 + SLC_BLOCK) - jnp.maximum(c0, s0), 0, None)
    overlap = overlap.astype(jnp.float32) / CMP_BLOCK
    imp = jnp.einsum('bghtn,ns->bgts', p_cmp, overlap)
    t = jnp.arange(T)[:, None]
    blk = jnp.arange(nslc)[None, :]
    cur = t // SLC_BLOCK
    visible = blk * SLC_BLOCK <= t
    forced = (blk == 0) | (blk == cur) | (blk == cur - 1)
    score = jnp.where(forced, FORCE, jnp.where(visible, imp, NEG))
    _, idx = lax.top_k(score, min(SLC_TOPN, nslc))
    return idx


def nsa_selected(q, k, v, idx):
    B, T, G, HPG, Dh = q.shape
    nslc = T // SLC_BLOCK
    nqb = T // Q_BLOCK
    n_sel = idx.shape[-1]
    kb = k.astype(jnp.float32).reshape(B, nslc, SLC_BLOCK, G, Dh).transpose(0, 3, 1, 2, 4)
    vb = v.astype(jnp.float32).reshape(B, nslc, SLC_BLOCK, G, Dh).transpose(0, 3, 1, 2, 4)
    qb = q.reshape(B, nqb, Q_BLOCK, G, HPG, Dh)
    ib = idx.reshape(B, G, nqb, Q_BLOCK, n_sel).transpose(0, 2, 1, 3, 4)
    t0s = jnp.arange(nqb) * Q_BLOCK

    def per_batch(args):
        q_b, k_b, v_b, i_b = args

        def per_qblock(a):
            qq, ii, t0 = a
            ksel = jax.vmap(lambda kg, ig: kg[ig])(k_b, ii)
            vsel = jax.vmap(lambda vg, ig: vg[ig])(v_b, ii)
            s = jnp.einsum('qghd,gqnld->gqhnl', qq, ksel)
            pos = ii[..., None] * SLC_BLOCK + jnp.arange(SLC_BLOCK)
            tq = t0 + jnp.arange(Q_BLOCK)
            valid = pos <= tq[None, :, None, None]
            s = jnp.where(valid[:, :, None], s, NEG)
            p = jax.nn.softmax(s.reshape(G, Q_BLOCK, HPG, n_sel * SLC_BLOCK), axis=-1).reshape(s.shape)
            return jnp.einsum('gqhnl,gqnld->qghd', p, vsel)

        return lax.map(per_qblock, (q_b, i_b, t0s))

    o = lax.map(per_batch, (qb, kb, vb, ib))
    return o.reshape(B, T, G, HPG, Dh)


def nsa_window(q, k, v):
    B, T, G, HPG, Dh = q.shape
    nqb = T // Q_BLOCK
    span = WINDOW + Q_BLOCK
    kp = jnp.pad(k.astype(jnp.float32), ((0, 0), (WINDOW, 0), (0, 0), (0, 0)))
    vp = jnp.pad(v.astype(jnp.float32), ((0, 0), (WINDOW, 0), (0, 0), (0, 0)))
    qb = q.reshape(B, nqb, Q_BLOCK, G, HPG, Dh).swapaxes(0, 1)
    t0s = jnp.arange(nqb) * Q_BLOCK

    def per_qblock(a):
        qq, t0 = a
        kk = lax.dynamic_slice_in_dim(kp, t0, span, axis=1)
        vv = lax.dynamic_slice_in_dim(vp, t0, span, axis=1)
        s = jnp.einsum('bqghd,bkgd->bghqk', qq, kk)
        kpos = t0 - WINDOW + jnp.arange(span)
        tq = t0 + jnp.arange(Q_BLOCK)
        valid = (kpos[None, :] <= tq[:, None]) & (kpos[None, :] > tq[:, None] - WINDOW) & (kpos[None, :] >= 0)
        p = jax.nn.softmax(jnp.where(valid, s, NEG), axis=-1)
        return jnp.einsum('bghqk,bkgd->bqghd', p, vv)

    o = lax.map(per_qblock, (qb, t0s))
    return o.swapaxes(0, 1).reshape(B, T, G, HPG, Dh)


def native_sparse_attention(q, kc_tok, vc_tok, k_slc, v_slc, k_win, v_win, gate_pre,
                            k_pe, k_w1, k_b1, k_w2, v_pe, v_w1, v_b1, v_w2):
    B, T, _ = q.shape
    G, HPG, Dh = NSA_KV_HEADS, NSA_HPG, NSA_DH
    qf = q.astype(jnp.float32).reshape(B, T, G, HPG, Dh) * Dh ** -0.5

    def kv(a):
        return a.reshape(B, T, G, Dh)

    kc = compress_blocks(kv(kc_tok), k_pe, k_w1, k_b1, k_w2)
    vc = compress_blocks(kv(vc_tok), v_pe, v_w1, v_b1, v_w2)
    o_cmp, p_cmp = nsa_compressed(qf, kc, vc)
    idx = select_blocks(p_cmp, T)
    o_slc = nsa_selected(qf, kv(k_slc), kv(v_slc), idx)
    o_win = nsa_window(qf, kv(k_win), kv(v_win))
    gates = jax.nn.sigmoid(gate_pre.astype(jnp.float32)).reshape(B, T, G, HPG, N_BRANCH)
    o = gates[..., 0:1] * o_cmp + gates[..., 1:2] * o_slc + gates[..., 2:3] * o_win
    return o.reshape(B, T, NSA_WIDTH).astype(q.dtype)


def hybrid_mixer(h, w_in, conv_w, ml_b_i, ml_b_f, ml_gn,
                 k_pe, k_w1, k_b1, k_w2, v_pe, v_w1, v_b1, v_w2, w_out):
    B, T, _ = h.shape
    proj = h @ w_in
    sizes = [ML_WIDTH] * 4 + [ML_HEADS] * 2 + [NSA_WIDTH] + [NSA_KV_WIDTH] * 6 + [NSA_HEADS * N_BRANCH]
    offs = np.cumsum(sizes)[:-1].tolist()
    mq, mk, mv, mo, mi, mf, nq, kc, vc, ks, vs, kw, vw, ng = jnp.split(proj, offs, axis=-1)

    qk = jax.nn.silu(causal_conv(jnp.concatenate([mq, mk], axis=-1), conv_w))
    mq, mk = jnp.split(qk, 2, axis=-1)

    def heads(a):
        return a.reshape(B, T, ML_HEADS, ML_DH)

    hm = mlstm_chunkwise(heads(mq), heads(mk), heads(mv), mi + ml_b_i, mf + ml_b_f)
    hm = head_rmsnorm(hm, ml_gn) * jax.nn.sigmoid(heads(mo).astype(jnp.float32))
    hm = hm.reshape(B, T, ML_WIDTH).astype(h.dtype)

    hn = native_sparse_attention(nq, kc, vc, ks, vs, kw, vw, ng,
                                 k_pe, k_w1, k_b1, k_w2, v_pe, v_w1, v_b1, v_w2)
    return jnp.concatenate([hm, hn], axis=-1) @ w_out


def setup_inputs(seed: int = 0) -> dict:
    key = jax.random.key(seed)
    ks = jax.random.split(key, 32)
    f32 = jnp.float32

    def nrm(k, shape, fan_in):
        return jax.random.normal(k, shape, f32) * fan_in ** -0.5

    def gain(k, shape):
        return 1.0 + 0.02 * jax.random.normal(k, shape, f32)

    def small(k, shape, scale):
        return scale * jax.random.normal(k, shape, f32)

    L = DEPTH
    fb = jnp.linspace(3.0, 6.0, ML_HEADS, dtype=f32)[None, :] + small(ks[9], (L, ML_HEADS), 0.1)
    return {
        'x': jax.random.normal(ks[0], (BATCH, SEQ, D_MODEL), f32),
        'ffn1_norm': gain(ks[1], (L, D_MODEL)),
        'ffn1_w1': nrm(ks[2], (L, D_MODEL, D_FF), D_MODEL),
        'ffn1_w3': nrm(ks[3], (L, D_MODEL, D_FF), D_MODEL),
        'ffn1_w2': nrm(ks[4], (L, D_FF, D_MODEL), D_FF),
        'mix_norm': gain(ks[5], (L, D_MODEL)),
        'w_in': nrm(ks[6], (L, D_MODEL, D_IN), D_MODEL),
        'conv_w': nrm(ks[7], (L, CONV_K, 2 * ML_WIDTH), CONV_K),
        'ml_b_i': small(ks[8], (L, ML_HEADS), 0.1),
        'ml_b_f': fb,
        'ml_gn': gain(ks[10], (L, ML_WIDTH)),
        'cmp_k_pe': small(ks[11], (L, CMP_BLOCK, NSA_DH), 0.02),
        'cmp_k_w1': nrm(ks[12], (L, CMP_BLOCK * NSA_DH, CMP_HIDDEN), CMP_BLOCK * NSA_DH),
        'cmp_k_b1': small(ks[13], (L, CMP_HIDDEN), 0.02),
        'cmp_k_w2': nrm(ks[14], (L, CMP_HIDDEN, NSA_DH), CMP_HIDDEN),
        'cmp_v_pe': small(ks[15], (L, CMP_BLOCK, NSA_DH), 0.02),
        'cmp_v_w1': nrm(ks[16], (L, CMP_BLOCK * NSA_DH, CMP_HIDDEN), CMP_BLOCK * NSA_DH),
        'cmp_v_b1': small(ks[17], (L, CMP_HIDDEN), 0.02),
        'cmp_v_w2': nrm(ks[18], (L, CMP_HIDDEN, NSA_DH), CMP_HIDDEN),
        'w_out': nrm(ks[19], (L, D_MIX, D_MODEL), D_MIX),
        'ffn2_norm': gain(ks[20], (L, D_MODEL)),
        'ffn2_w1': nrm(ks[21], (L, D_MODEL, D_FF), D_MODEL),
        'ffn2_w3': nrm(ks[22], (L, D_MODEL, D_FF), D_MODEL),
        'ffn2_w2': nrm(ks[23], (L, D_FF, D_MODEL), D_FF),
        'final_norm': gain(ks[24], (D_MODEL,)),
    }


def reference(x, ffn1_norm, ffn1_w1, ffn1_w3, ffn1_w2, mix_norm, w_in, conv_w, ml_b_i, ml_b_f,
              ml_gn, cmp_k_pe, cmp_k_w1, cmp_k_b1, cmp_k_w2, cmp_v_pe, cmp_v_w1, cmp_v_b1,
              cmp_v_w2, w_out, ffn2_norm, ffn2_w1, ffn2_w3, ffn2_w2, final_norm):
    for l in range(DEPTH):
        x = x + 0.5 * swiglu(rmsnorm(x, ffn1_norm[l]), ffn1_w1[l], ffn1_w3[l], ffn1_w2[l])
        h = rmsnorm(x, mix_norm[l])
        x = x + hybrid_mixer(h, w_in[l], conv_w[l], ml_b_i[l], ml_b_f[l], ml_gn[l],
                             cmp_k_pe[l], cmp_k_w1[l], cmp_k_b1[l], cmp_k_w2[l],
                             cmp_v_pe[l], cmp_v_w1[l], cmp_v_b1[l], cmp_v_w2[l], w_out[l])
        x = x + 0.5 * swiglu(rmsnorm(x, ffn2_norm[l]), ffn2_w1[l], ffn2_w3[l], ffn2_w2[l])
    return rmsnorm(x, final_norm)
```

```python
import numpy as np
from contextlib import ExitStack
import concourse.bass as bass
import concourse.mybir as mybir
from concourse.bass_utils import run_bass_kernel_spmd

F32 = mybir.dt.float32
BF16 = mybir.dt.bfloat16
AF = mybir.ActivationFunctionType
ALU = mybir.AluOpType

PE, ACT, DVE, POOL, SP = "tensor", "scalar", "vector", "gpsimd", "sync"
ENGINES = [PE, ACT, DVE, POOL, SP]
NDMA_SEM = 8

T = 2048
D = 1024
DFF = 2816
NT = 16
EPS = 1e-6
NEGBIG = -30000.0
DEBUG_BRANCHES = (1, 2)


class Buf:
    __slots__ = ("name", "w", "r")

    def __init__(self, name=""):
        self.name = name
        self.w = None
        self.r = []


class Op:
    __slots__ = ("eng", "idx", "fn", "deps", "is_dma", "sem", "val", "waited", "dma_prev")

    def __init__(self, eng, idx, fn, is_dma):
        self.eng = eng
        self.idx = idx
        self.fn = fn
        self.deps = {}
        self.is_dma = is_dma
        self.sem = None
        self.val = None
        self.waited = False
        self.dma_prev = None


class Prog:
    def __init__(self, nc):
        self.nc = nc
        self.ops = {e: [] for e in ENGINES}
        self.dma_count = {e: 0 for e in ENGINES}
        self.dma_last = {}
        self.final_ops = []

    def _add_dep(self, op, d):
        if d is None or d is op:
            return
        if d.is_dma:
            op.deps[("dma", id(d))] = d
            return
        if d.eng == PE and op.eng == PE and not op.is_dma:
            return
        k = ("eng", d.eng)
        cur = op.deps.get(k)
        if cur is None or cur.idx < d.idx:
            op.deps[k] = d

    def emit(self, eng, fn, reads=(), writes=(), dma=False, extra_deps=()):
        op = Op(eng, len(self.ops[eng]), fn, dma)
        for b in reads:
            self._add_dep(op, b.w)
        for b in writes:
            self._add_dep(op, b.w)
            for r in b.r:
                self._add_dep(op, r)
        for d in extra_deps:
            self._add_dep(op, d)
        for b in reads:
            b.r.append(op)
        for b in writes:
            b.w = op
            b.r = []
        if dma:
            n = self.dma_count[eng]
            self.dma_count[eng] = n + 1
            slot = n % NDMA_SEM
            prev = self.dma_last.get((eng, slot))
            op.dma_prev = prev
            self.dma_last[(eng, slot)] = op
            op.sem = ("dma", eng, slot)
            op.val = 16 * (n // NDMA_SEM + 1)
            if prev is not None:
                prev.waited = True
        for d in op.deps.values():
            d.waited = True
        self.ops[eng].append(op)
        return op

    def last_real(self, eng):
        for op in reversed(self.ops[eng]):
            if op.fn is not None:
                return op
        return None

    def barrier(self):
        lasts = [self.last_real(e) for e in ENGINES]
        dmas = list(self.dma_last.values())
        for e in ENGINES:
            self.emit(e, None, extra_deps=[l for l in lasts if l is not None] + dmas)

    def build(self, sems):
        for o in self.final_ops:
            o.waited = True
        for e in ENGINES:
            c = 0
            for op in self.ops[e]:
                if op.is_dma or op.fn is None:
                    continue
                if op.waited:
                    c += 1
                    op.sem = ("eng", e)
                    op.val = c
        self.sem_handles = sems

    def replay(self, engname, eng):
        sems = self.sem_handles
        cur_wait = {}

        def wait(d):
            if cur_wait.get(d.sem, 0) >= d.val:
                return
            cur_wait[d.sem] = d.val
            eng.wait_ge(sems[d.sem], d.val)

        for op in self.ops[engname]:
            if op.is_dma and op.dma_prev is not None:
                wait(op.dma_prev)
            for d in op.deps.values():
                wait(d)
            if op.fn is None:
                continue
            ins = op.fn(eng)
            if op.is_dma:
                ins.then_inc(sems[op.sem], 16)
            elif op.waited:
                ins.then_inc(sems[op.sem], 1)
        if engname == SP:
            for o in self.final_ops:
                wait(o)

    def sem_names(self):
        names = [("eng", e) for e in ENGINES]
        for e in ENGINES:
            if self.dma_count[e]:
                names += [("dma", e, s) for s in range(NDMA_SEM)]
        return names


_DTSIZE = {F32: 4, BF16: 2}


class Arena:
    def __init__(self, nc, base, cap):
        self.nc = nc
        self.base = base
        self.off = base
        self.cap = cap
        self.n = 0

    def alloc(self, name, shape, dt=F32):
        size = int(np.prod(shape[1:])) * _DTSIZE[dt]
        size = (size + 63) // 64 * 64
        assert self.off + size <= self.cap, f"SBUF arena overflow at {name}: {self.off + size} > {self.cap}"
        self.n += 1
        t = self.nc.alloc_sbuf_tensor_at(f"{name}_{self.n}_{self.off}", list(shape), dt, offset=self.off)
        self.off += size
        return t

    def mark(self):
        return self.off

    def reset(self, m):
        self.off = m


class Ring:
    def __init__(self, arena, name, shape, dt, n):
        self.items = [(arena.alloc(f"{name}{i}", shape, dt), Buf(f"{name}{i}")) for i in range(n)]
        self.i = 0

    def next(self):
        it = self.items[self.i % len(self.items)]
        self.i += 1
        return it


def _consts():
    c = {}
    s = np.arange(128)[:, None]
    t = np.arange(128)[None, :]
    c["c_ident"] = np.eye(128, dtype=np.float32)
    c["c_trile"] = (s <= t).astype(np.float32)
    c["c_trigt"] = (s > t).astype(np.float32)
    n = np.arange(128)[:, None]
    tt = np.arange(T)[None, :]
    cm = ((16 * n + 31) <= tt).astype(np.float32)
    cm[127] = 0.0
    c["c_cmpmask"] = cm
    b = np.arange(32)[:, None]
    col = np.arange(16 * 128)[None, :]
    c["c_ex"] = (b == (2 * (col // 128) + (col % 128) // 64)).astype(np.float32)
    c0 = np.arange(128)[:, None] * 16
    s0 = np.arange(32)[None, :] * 64
    ov = np.clip(np.minimum(c0 + 32, s0 + 64) - np.maximum(c0, s0), 0, None).astype(np.float32) / 32.0
    ov1 = np.concatenate([ov, np.ones((128, 1), np.float32)], axis=1)
    ov1[127] = 0.0
    c["c_ov1"] = ov1
    tq = np.arange(T)[:, None]
    blk = np.arange(32)[None, :]
    cur = tq // 64
    visible = blk * 64 <= tq
    forced = (blk == 0) | (blk == cur) | (blk == cur - 1)
    vis01 = (visible & ~forced).astype(np.float32)
    fval = np.where(blk == 0, 3e30, np.where(blk == cur, 2e30, 1e30))
    add = np.where(forced, fval, np.where(visible, 0.0, -1e30 - blk * 1e28)).astype(np.float32)
    c["c_selvis"] = vis01
    c["c_seladd"] = add
    r = np.arange(4)[:, None]
    cc = np.arange(4 * 128)[None, :]
    c["c_sel4"] = (r == cc // 128).astype(np.float32)
    c["c_ident4"] = np.eye(4, dtype=np.float32)
    return c


def build_program(nseq, debug=(), do_mixer=True):
    nc = bass.Bass("TRN2", target_bir_lowering=False)
    P = Prog(nc)

    def din(name, shape):
        return nc.dram_tensor(name, list(shape), F32, kind="ExternalInput").ap()

    x_d = din("x", [nseq, T, D])
    out_d = nc.dram_tensor("out", [nseq, T, D], F32, kind="ExternalOutput").ap()
    xst_d = nc.dram_tensor("xstash", [T, D], F32, kind="Internal").ap()
    nm_d = nc.dram_tensor("nmrows", [4, T], F32, kind="Internal").ap()
    wd = {}
    for ff in ("ffn1", "ffn2"):
        wd[ff + "_w1"] = din(ff + "_w1", [D, DFF])
        wd[ff + "_w3"] = din(ff + "_w3", [D, DFF])
        wd[ff + "_w2"] = din(ff + "_w2", [DFF, D])
    for nm in ("ffn1_norm", "mix_norm", "ffn2_norm", "final_norm"):
        wd[nm] = din(nm, [1, D])
    wd["w_gi"] = din("w_gi", [D, 4])
    wd["w_gf"] = din("w_gf", [D, 4])
    wd["w_mq"] = din("w_mq", [D, 512])
    wd["w_mk"] = din("w_mk", [D, 512])
    wd["w_mv"] = din("w_mv", [D, 512])
    wd["w_mo"] = din("w_mo", [D, 512])
    wd["w_nq"] = din("w_nq", [D, 512])
    wd["w_ksd"] = din("w_ksd", [D, 256])
    wd["w_kwd"] = din("w_kwd", [D, 256])
    wd["w_kvc2"] = din("w_kvc2", [D, 512])
    wd["w_vtok"] = din("w_vtok", [D, 280])
    wd["conv_wT"] = din("conv_wT", [D, 4])
    wd["ml_b_i"] = din("ml_b_i", [4, 1])
    wd["ml_b_f"] = din("ml_b_f", [4, 1])
    wd["ml_gn"] = din("ml_gn", [1, 512])
    for kv in ("k", "v"):
        wd[f"cmp_{kv}_peT2"] = din(f"cmp_{kv}_peT2", [128, 16])
        wd[f"cmp_{kv}_w1"] = din(f"cmp_{kv}_w1", [2048, 256])
        wd[f"cmp_{kv}_b1"] = din(f"cmp_{kv}_b1", [128, 2])
    wd["cmp_k_w2d"] = din("cmp_k_w2d", [256, 128])
    wd["cmp_v_w2"] = din("cmp_v_w2", [256, 64])
    wd["w_out"] = din("w_out", [D, D])
    cd = {}
    for nm, arr in _consts().items():
        cd[nm] = din(nm, arr.shape)
    dbg = {}
    for nm, shape in debug:
        dbg[nm] = nc.dram_tensor(nm, list(shape), F32, kind="ExternalOutput").ap()

    es = ExitStack()
    with es:
        SB_TOTAL = 224 * 1024
        base = (SB_TOTAL - int(nc.sbuf_bytes_remaining) + 127) // 64 * 64
        ar = Arena(nc, base, SB_TOTAL - 64)
        banks = [(es.enter_context(nc.psum_tensor(f"bank{i}", [128, 512], F32)), Buf(f"bank{i}")) for i in range(8)]
        bank_i = [0]
        held = set()

        def nextbank(hold=False):
            while True:
                idx = bank_i[0] % 8
                bank_i[0] += 1
                if idx not in held:
                    break
            if hold:
                held.add(idx)
            return banks[idx]

        def release(bb):
            for idx, (_, b) in enumerate(banks):
                if b is bb:
                    held.discard(idx)

        ident_b = ar.alloc("ident_b", [128, 128], BF16)
        trile = ar.alloc("trile", [128, 128], BF16)
        trigt = ar.alloc("trigt", [128, 128], BF16)
        cmpmask = ar.alloc("cmpmask", [128, T], BF16)
        exm = ar.alloc("exm", [32, 16 * 128], BF16)
        ov1 = ar.alloc("ov1", [128, 33], BF16)
        selvis = ar.alloc("selvis", [128, NT, 32], F32)
        seladd = ar.alloc("seladd", [128, NT, 32], F32)
        sel4 = ar.alloc("sel4", [4, 512], F32)
        ident4 = ar.alloc("ident4", [4, 4], F32)
        gbc = ar.alloc("gbc", [128, D], F32)
        gnbc = ar.alloc("gnbc", [128, 512], F32)
        convw = ar.alloc("convw", [128, 8, 4], F32)
        b_i = ar.alloc("b_i", [4, 1], F32)
        b_f = ar.alloc("b_f", [4, 1], F32)
        nb_f = ar.alloc("nb_f", [4, 1], F32)
        epsb = ar.alloc("epsb", [128, 1], F32)
        ss = ar.alloc("ss", [128, NT], F32)
        rstd = ar.alloc("rstd", [128, NT], F32)
        B_const = Buf("const")
        B_gbc = Buf("gbc")
        B_ss = Buf("ss")
        B_rstd = Buf("rstd")
        xs_off = ar.mark()
        xs = ar.alloc("xs", [128, NT, D], F32)
        B_x = [Buf(f"x{i}") for i in range(NT)]
        hT = ar.alloc("hT", [128, 8, T], BF16)
        B_hT = [Buf(f"hT{i}") for i in range(4)]
        B_stash = Buf("stash")
        norm_junk = Ring(ar, "nj", [128, D], BF16, 1)
        norm_hn = Ring(ar, "nh", [128, D], BF16, 2)
        phase_mark = ar.mark()

        def pool_cast_load(dst, src, writes):
            return P.emit(POOL, lambda e: e.dma_start(out=dst, in_=src), writes=writes, dma=True)

        def sp_load(dst, src, writes, reads=()):
            return P.emit(SP, lambda e: e.dma_start(out=dst, in_=src), writes=writes, reads=reads, dma=True)

        pool_cast_load(ident_b[:], cd["c_ident"][:, :], [B_const])
        P.emit(DVE, lambda e: e.memset(epsb[:], EPS), writes=[B_const])
        consts_loaded = [False]

        def load_consts_late():
            if consts_loaded[0]:
                return
            consts_loaded[0] = True
            pool_cast_load(trile[:], cd["c_trile"][:, :], [B_const])
            pool_cast_load(trigt[:], cd["c_trigt"][:, :], [B_const])
            pool_cast_load(cmpmask[:], cd["c_cmpmask"][:, :], [B_const])
            pool_cast_load(exm[:], cd["c_ex"][:, :], [B_const])
            pool_cast_load(ov1[:], cd["c_ov1"][:, :], [B_const])
            sp_load(selvis[:], cd["c_selvis"].rearrange("(i p) b -> p i b", p=128), [B_const])
            sp_load(seladd[:], cd["c_seladd"].rearrange("(i p) b -> p i b", p=128), [B_const])
            sp_load(sel4[:], cd["c_sel4"][:, :], [B_const])
            sp_load(ident4[:], cd["c_ident4"][:, :], [B_const])
            sp_load(gnbc[:], bass.AP(wd["ml_gn"].tensor, 0, [[0, 128], [1, 512]]), [B_const])
            sp_load(convw[:], wd["conv_wT"].rearrange("(k p) j -> p k j", p=128), [B_const])
            sp_load(b_i[:], wd["ml_b_i"][:, :], [B_const])
            sp_load(b_f[:], wd["ml_b_f"][:, :], [B_const])
            P.emit(DVE, lambda e: e.tensor_scalar(out=nb_f[:], in0=b_f[:], scalar1=-1.0, scalar2=None, op0=ALU.mult),
                   reads=[B_const], writes=[B_const])

        def load_gain(name):
            sp_load(gbc[:], bass.AP(wd[name].tensor, 0, [[0, 128], [1, D]]), [B_gbc])

        def rms_rstd_grp(width_scale, i0, n):
            P.emit(ACT, lambda e: e.activation(out=rstd[:, i0:i0 + n], in_=ss[:, i0:i0 + n], func=AF.Sqrt, bias=epsb[:, 0:1], scale=width_scale),
                   reads=[B_ss, B_const], writes=[B_rstd])
            P.emit(DVE, lambda e: e.reciprocal(out=rstd[:, i0:i0 + n], in_=rstd[:, i0:i0 + n]), reads=[B_rstd], writes=[B_rstd])

        def norm_to_hT(tmp_arena):
            norm_begin()
            for i0 in range(0, NT, 4):
                norm_group(i0)

        def norm_begin():
            P.emit(DVE, lambda e: e.memset(ss[:], 0.0), writes=[B_ss])

        def norm_group(i0):
            junk, hn = norm_junk, norm_hn
            if True:
                for i in range(i0, i0 + 4):
                    jt, jb = junk.next()
                    P.emit(ACT, lambda e, i=i, jt=jt: e.activation(out=jt[:], in_=xs[:, i, :], func=AF.Square,
                                                                   accum_out=ss[:, i:i + 1]),
                           reads=[B_x[i]], writes=[jb, B_ss])
                rms_rstd_grp(1.0 / D, i0, 4)
                for i in range(i0, i0 + 4):
                    ht, hb = hn.next()
                    P.emit(DVE, lambda e, i=i, ht=ht: e.scalar_tensor_tensor(out=ht[:], in0=xs[:, i, :], scalar=rstd[:, i:i + 1],
                                                                             in1=gbc[:], op0=ALU.mult, op1=ALU.mult),
                           reads=[B_x[i], B_rstd, B_gbc], writes=[hb])
                    bk, bb = nextbank()
                    bkb = bk[:, :].bitcast(BF16)
                    for k in range(8):
                        P.emit(PE, lambda e, k=k, ht=ht, bkb=bkb: e.transpose(bkb[:, k * 128:(k + 1) * 128], ht[:, k * 128:(k + 1) * 128], ident_b[:]),
                               reads=[hb, B_const], writes=[bb])
                    P.emit(ACT, lambda e, i=i, bkb=bkb: e.copy(out=hT[:, :, i * 128:(i + 1) * 128],
                                                               in_=bkb.rearrange("p (k t) -> p k t", k=8)),
                           reads=[bb], writes=[B_hT[i // 4]])

        ffn_bufs = {}

        def ffn(w1_d, w3_d, w2_d, tmp_arena, pre_tg=None, post_tile=None):
            groups = [(f0, min(512, DFF - f0)) for f0 in range(0, DFF, 512)]
            if not ffn_bufs:
                tmp_arena.reset(phase_mark)
                ffn_bufs["w1r"] = Ring(tmp_arena, "w1g", [128, 8, 512], BF16, 2)
                ffn_bufs["w3r"] = Ring(tmp_arena, "w3g", [128, 8, 512], BF16, 2)
                ffn_bufs["w2r"] = Ring(tmp_arena, "w2g", [128, 4, D], BF16, 2)
                ffn_bufs["aT"] = tmp_arena.alloc("aT", [128, 4, T], BF16)
                ffn_bufs["B_aT"] = [[Buf() for _ in range(4)] for _ in range(4)]
                ffn_bufs["sil"] = Ring(tmp_arena, "sil", [128, 512], F32, 2)
                ffn_bufs["ot"] = Ring(tmp_arena, "fo", [128, D], F32, 2)
            w1r, w3r, w2r = ffn_bufs["w1r"], ffn_bufs["w3r"], ffn_bufs["w2r"]
            aT, B_aT, sil = ffn_bufs["aT"], ffn_bufs["B_aT"], ffn_bufs["sil"]
            w1v = w1_d.rearrange("(k p) f -> p k f", p=128)
            w3v = w3_d.rearrange("(k p) f -> p k f", p=128)
            w2v = w2_d.rearrange("(c p) d -> p c d", p=128)

            def load_group(g):
                f0, fw = groups[g]
                nfc = fw // 128
                t1, b1 = w1r.next()
                t3, b3 = w3r.next()
                t2, b2 = w2r.next()
                pool_cast_load(t1[:, :, 0:fw], w1v[:, :, f0:f0 + fw], [b1])
                pool_cast_load(t3[:, :, 0:fw], w3v[:, :, f0:f0 + fw], [b3])
                pool_cast_load(t2[:, 0:nfc, :], w2v[:, f0 // 128:f0 // 128 + nfc, :], [b2])
                return (t1, b1, t3, b3, t2, b2)

            cur = load_group(0)
            for g, (f0, fw) in enumerate(groups):
                nxt = load_group(g + 1) if g + 1 < len(groups) else None
                if g == len(groups) - 1:
                    load_consts_late()
                t1, b1, t3, b3, t2, b2 = cur
                nfc = fw // 128
                order = [(fc, tg) for fc in range(nfc) for tg in range(4)]
                if g == 0 and pre_tg is not None:
                    order = [(fc, tg) for tg in range(4) for fc in range(nfc)]
                for fc, tg in order:
                    if g == 0 and pre_tg is not None and fc == 0:
                        pre_tg(tg)
                    if True:
                        bu, bbu = nextbank()
                        bv, bbv = nextbank()
                        for k in range(8):
                            P.emit(PE, lambda e, k=k, fc=fc, tg=tg, bu=bu, t1=t1: e.matmul(
                                bu[:, :], lhsT=t1[:, k, fc * 128:(fc + 1) * 128], rhs=hT[:, k, tg * 512:(tg + 1) * 512],
                                start=(k == 0), stop=(k == 7)), reads=[b1, B_hT[tg]], writes=[bbu])
                        for k in range(8):
                            P.emit(PE, lambda e, k=k, fc=fc, tg=tg, bv=bv, t3=t3: e.matmul(
                                bv[:, :], lhsT=t3[:, k, fc * 128:(fc + 1) * 128], rhs=hT[:, k, tg * 512:(tg + 1) * 512],
                                start=(k == 0), stop=(k == 7)), reads=[b3, B_hT[tg]], writes=[bbv])
                        st, sbf = sil.next()
                        P.emit(ACT, lambda e, bu=bu, st=st: e.activation(out=st[:], in_=bu[:, :], func=AF.Silu),
                               reads=[bbu], writes=[sbf])
                        P.emit(DVE, lambda e, fc=fc, tg=tg, st=st, bv=bv: e.tensor_tensor(
                            out=aT[:, fc, tg * 512:(tg + 1) * 512], in0=st[:], in1=bv[:, :], op=ALU.mult),
                            reads=[sbf, bbv], writes=[B_aT[fc][tg]])
                for i in range(NT):
                    by0, bby0 = nextbank()
                    by1, bby1 = nextbank()
                    for half, (by, bby) in enumerate(((by0, bby0), (by1, bby1))):
                        for fc in range(nfc):
                            P.emit(PE, lambda e, fc=fc, i=i, half=half, by=by, t2=t2, nfc=nfc: e.matmul(
                                by[:, :], lhsT=aT[:, fc, i * 128:(i + 1) * 128], rhs=t2[:, fc, half * 512:(half + 1) * 512],
                                start=(fc == 0), stop=(fc == nfc - 1)), reads=[B_aT[fc][i // 4], b2], writes=[bby])
                        P.emit(DVE, lambda e, i=i, half=half, by=by: e.scalar_tensor_tensor(
                            out=xs[:, i, half * 512:(half + 1) * 512], in0=by[:, :], scalar=0.5,
                            in1=xs[:, i, half * 512:(half + 1) * 512], op0=ALU.mult, op1=ALU.add),
                            reads=[bby, B_x[i]], writes=[B_x[i]])
                    if post_tile is not None and g == len(groups) - 1:
                        post_tile(i)
                cur = nxt

        def dump(name, src_ap, reads):
            if name in dbg:
                P.final_ops.append(P.emit(POOL, lambda e: e.dma_start(out=dbg[name], in_=src_ap), reads=reads, dma=True))


        xstv = xst_d.rearrange("(i p) d -> p i d", p=128)

        def stash_tile(i):
            P.emit(SP, lambda e, i=i: e.dma_start(out=xstv[:, i, :], in_=xs[:, i, :]), reads=[B_x[i]],
                   writes=[B_stash], dma=True)

        def mixer(s):
            SC_ML = 128.0 ** -0.5
            load_gain("mix_norm")
            norm_to_hT(ar)
            P.barrier()
            arx = Arena(nc, xs_off, xs_off + NT * D * 4)
            m_ar0 = ar.mark()
            mixcat = ar.alloc("mixcat", [128, NT, D], BF16)
            B_mix = [Buf() for _ in range(NT)]
            wring = Ring(ar, "wb", [128, 8, 512], BF16, 2)
            ptring = Ring(ar, "pt", [128, 512], BF16, 4)
            sm = Ring(ar, "sm", [128, 16], F32, 12)
            junk = ar.alloc("mjunk", [128, 128], F32)
            B_junk = Buf()
            m_ar1 = ar.mark()

            LOOK = 2

            deferred = []

            def defer(delay, fn):
                deferred.append([delay, fn])

            def tick():
                for d in deferred:
                    d[0] -= 1
                ready = [d for d in deferred if d[0] <= 0]
                for d in ready:
                    deferred.remove(d)
                for d in ready:
                    d[1]()

            def run_pipeline(steps, fA, fC, look=LOOK, fillers=()):
                n = len(steps)
                fillers = list(fillers)
                for t in range(n + look):
                    if t < n:
                        fA(*steps[t])
                    if t - look >= 0:
                        fC(*steps[t - look])
                    if fillers:
                        fillers.pop(0)()
                    tick()
                while fillers:
                    fillers.pop(0)()
                    tick()
                while deferred:
                    tick()

            def load_w(name, ncols, c0=0):
                t, b = wring.next()
                src = wd[name].rearrange("(k p) m -> p k m", p=128)[:, :, c0:c0 + ncols]
                pool_cast_load(t[:, :, 0:ncols], src, [b])
                return t, b

            def proj_fm(wt, wb, c0, M, evac):
                for tg in range(4):
                    bk, bb = nextbank()
                    for k in range(8):
                        P.emit(PE, lambda e, k=k, tg=tg, bk=bk: e.matmul(
                            bk[0:M, :], lhsT=wt[:, k, c0:c0 + M], rhs=hT[:, k, tg * 512:(tg + 1) * 512],
                            start=(k == 0), stop=(k == 7)), reads=[wb, B_hT[tg]], writes=[bb])
                    evac(tg, bk, bb)

            def proj_tm(wt, wb, c0, N, evac):
                for i in range(NT):
                    bk, bb = nextbank()
                    for k in range(8):
                        P.emit(PE, lambda e, k=k, i=i, bk=bk: e.matmul(
                            bk[:, 0:N], lhsT=hT[:, k, i * 128:(i + 1) * 128], rhs=wt[:, k, c0:c0 + N],
                            start=(k == 0), stop=(k == 7)), reads=[wb, B_hT[i // 4]], writes=[bb])
                    evac(i, bk, bb)

            rowI = ar.alloc("rowI", [4, T], F32)
            rowL = ar.alloc("rowL", [4, T], F32)
            rowA = ar.alloc("rowA", [4, T], F32)
            B_rI, B_rL, B_rA = Buf(), Buf(), Buf()
            ztok = ar.alloc("ztok", [128, NT, 4], F32)
            ecltok = ar.alloc("ecltok", [128, NT, 4], F32)
            B_zt, B_et = Buf(), Buf()
            nmring = Ring(ar, "nmb", [128, 512], F32, 1)
            nj_t, nj_b = norm_junk.items[0]
            nmring.items.append((nj_t[:, :].bitcast(F32), nj_b))
            B_nmd = Buf("nmd")
            nm_state = {}
            accring = Ring(ar, "accs", [128, 4, 129], F32, 2)

            wt, wb = load_w("w_gi", 4)
            proj_fm(wt, wb, 0, 4, lambda tg, bk, bb: P.emit(ACT, lambda e: e.activation(
                out=rowI[:, tg * 512:(tg + 1) * 512], in_=bk[0:4, :], func=AF.Identity, bias=b_i[:, 0:1], scale=1.0),
                reads=[bb, B_const], writes=[B_rI]))
            wt, wb = load_w("w_gf", 4)
            proj_fm(wt, wb, 0, 4, lambda tg, bk, bb: P.emit(ACT, lambda e: e.activation(
                out=rowL[:, tg * 512:(tg + 1) * 512], in_=bk[0:4, :], func=AF.Exp, bias=nb_f[:, 0:1], scale=-1.0),
                reads=[bb, B_const], writes=[B_rL]))
            P.emit(ACT, lambda e: e.activation(out=rowL[:], in_=rowL[:], func=AF.Ln, bias=1.0, scale=1.0),
                   reads=[B_rL], writes=[B_rL])
            P.emit(DVE, lambda e: e.tensor_tensor_scan(out=rowA[:], data0=rowL[:], data1=rowL[:], initial=0.0,
                                                       op0=ALU.add, op1=ALU.max), reads=[B_rL], writes=[B_rA])
            P.emit(DVE, lambda e: e.tensor_tensor(out=rowI[:], in0=rowI[:], in1=rowA[:], op=ALU.add),
                   reads=[B_rI, B_rA], writes=[B_rI])
            P.emit(DVE, lambda e: e.tensor_tensor_scan(out=rowL[:], data0=rowI[:], data1=rowI[:], initial=0.0,
                                                       op0=ALU.max, op1=ALU.max), reads=[B_rI, B_rL], writes=[B_rL])
            P.emit(DVE, lambda e: e.tensor_scalar(out=rowL[:], in0=rowL[:], scalar1=-1.0, scalar2=None, op0=ALU.mult),
                   reads=[B_rL], writes=[B_rL])
            P.emit(SP, lambda e: e.dma_start(out=nm_d[:, :], in_=rowL[:]), reads=[B_rL], writes=[B_nmd], dma=True)

            def nm_fetch(h, I):
                if (h, I) in nm_state or h > 3:
                    return
                nmb, nmbb = nmring.next()
                src = bass.AP(nm_d.tensor, h * T + I * 512, [[0, 128], [1, 512]])
                P.emit(SP, lambda e: e.dma_start(out=nmb[:, :], in_=src),
                       reads=[B_nmd], writes=[nmbb], dma=True)
                nm_state[(h, I)] = (nmb, nmbb)
            P.emit(DVE, lambda e: e.tensor_tensor(out=rowA[:], in0=rowA[:], in1=rowL[:], op=ALU.add),
                   reads=[B_rA, B_rL], writes=[B_rA])
            Vp = arx.alloc("Vp", [128, NT, 4, 130], BF16)
            B_Vp = [Buf() for _ in range(NT)]
            sigmo = arx.alloc("sigmo", [128, NT, 512], BF16)
            B_sig = [Buf() for _ in range(NT)]
            qpre = arx.alloc("qpre", [128, T + 3], F32)
            cacc = arx.alloc("cacc", [128, T], F32)
            qT = arx.alloc("qT", [128, T], BF16)
            kT = arx.alloc("kT", [128, T], BF16)
            B_qpre, B_cacc, B_qT, B_kT = Buf(), Buf(), Buf(), Buf()
            wgring = Ring(arx, "wg", [128, 512], BF16, 4)
            sgring = Ring(arx, "sg", [128, 512], F32, 1)
            P.emit(POOL, lambda e: e.memset(Vp[:], 1.0), writes=B_Vp)
            wt, wb = load_w("w_mv", 512)
            proj_tm(wt, wb, 0, 512, lambda i, bk, bb: P.emit(ACT, lambda e: e.copy(
                out=Vp[:, i, :, 0:128], in_=bk[:, :].rearrange("p (h d) -> p h d", h=4)), reads=[bb], writes=[B_Vp[i]]))
            wt, wb = load_w("w_mo", 512)

            def ev_o(i, bk, bb):
                st, sb_ = sgring.next()
                P.emit(ACT, lambda e: e.activation(out=st[:], in_=bk[:, :], func=AF.Sigmoid), reads=[bb], writes=[sb_])
                P.emit(DVE, lambda e: e.tensor_tensor(out=sigmo[:, i, :], in0=st[:], in1=gnbc[:], op=ALU.mult),
                       reads=[sb_, B_const], writes=[B_sig[i]])
            proj_tm(wt, wb, 0, 512, ev_o)
            P.emit(ACT, lambda e: e.activation(out=rowA[:], in_=rowA[:], func=AF.Exp), reads=[B_rA], writes=[B_rA])
            for (row, B_row, tok, B_tok) in ((rowI, B_rI, ztok, B_zt), (rowA, B_rA, ecltok, B_et)):
                bk, bb = nextbank()
                for i in range(NT):
                    P.emit(PE, lambda e, i=i, bk=bk, row=row: e.transpose(bk[:, i * 4:(i + 1) * 4], row[0:4, i * 128:(i + 1) * 128], ident4[:]),
                           reads=[B_row, B_const], writes=[bb])
                P.emit(DVE, lambda e, bk=bk, tok=tok: e.tensor_copy(out=tok[:].rearrange("p i h -> p (i h)"), in_=bk[:, 0:64]),
                       reads=[bb], writes=[B_tok])


            for h in range(4):
                for (wname, dstT, B_dst, cbase) in (("w_mq", qT, B_qT, 0), ("w_mk", kT, B_kT, 4)):
                    wt, wb = load_w(wname, 128, c0=h * 128)
                    P.emit(DVE, lambda e: e.memset(qpre[:, 0:3], 0.0), writes=[B_qpre])
                    proj_fm(wt, wb, 0, 128, lambda tg, bk, bb: P.emit(ACT, lambda e: e.copy(
                        out=qpre[:, 3 + tg * 512:3 + (tg + 1) * 512], in_=bk[:, :]), reads=[bb], writes=[B_qpre]))
                    cw = convw[:, cbase + h, :]
                    P.emit(DVE, lambda e, cw=cw: e.tensor_scalar(out=cacc[:], in0=qpre[:, 3:3 + T], scalar1=cw[:, 3:4],
                                                                 scalar2=None, op0=ALU.mult),
                           reads=[B_qpre, B_const], writes=[B_cacc])
                    for j in (2, 1, 0):
                        P.emit(DVE, lambda e, cw=cw, j=j: e.scalar_tensor_tensor(
                            out=cacc[:], in0=qpre[:, j:j + T], scalar=cw[:, j:j + 1], in1=cacc[:],
                            op0=ALU.mult, op1=ALU.add), reads=[B_qpre, B_cacc, B_const], writes=[B_cacc])
                    P.emit(ACT, lambda e, dstT=dstT: e.activation(out=dstT[:], in_=cacc[:], func=AF.Silu),
                           reads=[B_cacc], writes=[B_dst])
                steps = []
                for I in range(4):
                    st_I = {}
                    for j in range(4 * I + 4):
                        steps.append((I, j, st_I))

                def ml_A(I, j, st, h=h):
                    if j == 0:
                        nm_fetch(h, I)
                        st["nmb"] = nm_state[(h, I)]
                        if I + 1 < 4:
                            nm_fetch(h, I + 1)
                        else:
                            nm_fetch(h + 1, 0)
                    nmb, nmbb = st["nmb"]
                    qd = j - 4 * I
                    c0 = max(0, qd) * 128
                    bs, bbs = nextbank()
                    P.emit(PE, lambda e, bs=bs: e.matmul(bs[:, c0:512], lhsT=kT[:, j * 128:(j + 1) * 128],
                                                         rhs=qT[:, I * 512 + c0:(I + 1) * 512], start=True, stop=True),
                           reads=[B_kT, B_qT], writes=[bbs])
                    wg, wgb = wgring.next()
                    P.emit(ACT, lambda e, wg=wg, nmb=nmb: e.activation(
                        out=wg[:, c0:512], in_=nmb[:, c0:512], func=AF.Exp, bias=ztok[:, j, h:h + 1], scale=1.0),
                        reads=[nmbb, B_zt], writes=[wgb])
                    pt, pb = ptring.next()
                    P.emit(DVE, lambda e, pt=pt, bs=bs, wg=wg: e.scalar_tensor_tensor(
                        out=pt[:, c0:512], in0=bs[:, c0:512], scalar=SC_ML, in1=wg[:, c0:512], op0=ALU.mult, op1=ALU.mult),
                        reads=[bbs, wgb], writes=[pb])
                    if qd >= 0:
                        P.emit(DVE, lambda e, pt=pt: e.tensor_tensor(
                            out=pt[:, qd * 128:(qd + 1) * 128], in0=pt[:, qd * 128:(qd + 1) * 128], in1=trile[:],
                            op=ALU.mult), reads=[pb, B_const], writes=[pb])
                    st[("pt", j)] = (pt, pb)

                def ml_C(I, j, st, h=h):
                    pt, pb = st.pop(("pt", j))
                    if j == 0:
                        st["accb"] = [nextbank(hold=True), nextbank(hold=True)]
                        st["fresh"] = [True, True]
                    accb = st["accb"]
                    qd = j - 4 * I
                    for q in range(max(0, qd), 4):
                        ab, abb = accb[q // 2]
                        o0 = (q % 2) * 129
                        first = st["fresh"][q // 2]
                        st["fresh"][q // 2] = False
                        P.emit(PE, lambda e, ab=ab, o0=o0, pt=pt, q=q, first=first: e.matmul(
                            ab[:, o0:o0 + 129], lhsT=pt[:, q * 128:(q + 1) * 128], rhs=Vp[:, j, h, 0:129],
                            start=first, stop=(j == 4 * I + q), skip_group_check=True), reads=[pb, B_Vp[j]], writes=[abb])
                    if j != 4 * I + 3:
                        return
                    acs, acsb = accring.next()
                    for half in range(2):
                        ab, abb = accb[half]
                        P.emit(DVE, lambda e, ab=ab, half=half, acs=acs: e.tensor_copy(
                            out=acs[:, 2 * half:2 * half + 2, :], in_=ab[:, 0:258].rearrange("p (q c) -> p q c", c=129)),
                            reads=[abb], writes=[acsb])
                        release(abb)
                    d1, d1b = sm.next()
                    d2, d2b = sm.next()
                    d3, d3b = sm.next()
                    i0 = 4 * I

                    def s1():
                        P.emit(DVE, lambda e: e.tensor_scalar(out=d1[:, 0:4], in0=acs[:, :, 128], scalar1=-1.0, scalar2=None, op0=ALU.mult),
                               reads=[acsb], writes=[d1b])
                        P.emit(DVE, lambda e: e.tensor_tensor(out=d1[:, 0:4], in0=d1[:, 0:4], in1=acs[:, :, 128], op=ALU.max),
                               reads=[acsb, d1b], writes=[d1b])
                        P.emit(DVE, lambda e: e.tensor_tensor(out=d1[:, 0:4], in0=d1[:, 0:4], in1=ecltok[:, i0:i0 + 4, h], op=ALU.max),
                               reads=[d1b, B_et], writes=[d1b])
                        P.emit(DVE, lambda e: e.reciprocal(out=d1[:, 0:4], in_=d1[:, 0:4]), reads=[d1b], writes=[d1b])
                        P.emit(DVE, lambda e: e.memset(d2[:, 0:4], 0.0), writes=[d2b])

                    def s2():
                        for q in range(4):
                            P.emit(ACT, lambda e, q=q: e.activation(
                                out=junk[:], in_=acs[:, q, 0:128], func=AF.Square, scale=d1[:, q:q + 1], accum_out=d2[:, q:q + 1]),
                                reads=[acsb, d1b], writes=[B_junk, d2b])

                    def s3():
                        P.emit(ACT, lambda e: e.activation(out=d3[:, 0:4], in_=d2[:, 0:4], func=AF.Ln, bias=epsb[:, 0:1], scale=1.0 / 128),
                               reads=[d2b, B_const], writes=[d3b])
                        P.emit(ACT, lambda e: e.activation(out=d3[:, 0:4], in_=d3[:, 0:4], func=AF.Exp, scale=-0.5),
                               reads=[d3b], writes=[d3b])

                    def s4():
                        P.emit(DVE, lambda e: e.tensor_tensor(out=d3[:, 0:4], in0=d3[:, 0:4], in1=d1[:, 0:4], op=ALU.mult),
                               reads=[d3b, d1b], writes=[d3b])
                        for q in range(4):
                            i = i0 + q
                            P.emit(DVE, lambda e, q=q, i=i: e.scalar_tensor_tensor(
                                out=mixcat[:, i, h * 128:(h + 1) * 128], in0=acs[:, q, 0:128], scalar=d3[:, q:q + 1],
                                in1=sigmo[:, i, h * 128:(h + 1) * 128], op0=ALU.mult, op1=ALU.mult),
                                reads=[acsb, d3b, B_sig[i]], writes=[B_mix[i]])
                    defer(1, s1)
                    defer(2, s2)
                    defer(3, s3)
                    defer(4, s4)

                run_pipeline(steps, ml_A, ml_C, look=3)
            dump("d_ztok", ztok[:].rearrange("p i h -> p (i h)"), [B_zt])
            dump("d_ecl", ecltok[:].rearrange("p i h -> p (i h)"), [B_et])
            dump("d_qT", qT[:], [B_qT])
            dump("d_kT", kT[:], [B_kT])
            dump("d_nm", rowL[:], [B_rL])
            P.barrier()
            ar.reset(m_ar1)
            arx.reset(xs_off)

            SC_N = 0.125
            VV = arx.alloc("VV", [128, NT, 4, 66], BF16)
            m_arx1 = arx.mark()
            B_VV = [Buf() for _ in range(NT)]
            gates = ar.alloc("gates", [128, NT, 24], F32)
            B_gates = Buf()
            kcmpT = [ar.alloc(f"kcmpT{g}", [128, 128], BF16) for g in range(2)]
            Rg = [ar.alloc(f"Rg{g}", [128, 97], BF16) for g in range(2)]
            B_kcmp = [Buf(), Buf()]
            B_Rg = [Buf(), Buf()]
            B_hn = [Buf() for _ in range(NT)]
            imp = ar.alloc("imp", [128, NT, 32], F32)
            B_imp = [Buf() for _ in range(NT)]
            selT = ar.alloc("selT", [32, T], BF16)
            B_nst = [Buf() for _ in range(4)]
            scr = Ring(ar, "scr", [128, 32], F32, 4)
            nsr = Ring(ar, "nsr", [128, 32], BF16, 8)
            hidT = [ar.alloc(f"hidT{i}", [128, 2, 128], BF16) for i in range(2)]
            B_hid = [Buf(), Buf()]
            biasT = ar.alloc("biasT", [128, 2], F32)
            B_bias = Buf()
            b1t = ar.alloc("b1t", [128, 2], F32)
            B_b1t = Buf()
            w2d = ar.alloc("w2d", [128, 2, 128], BF16)
            w2v = ar.alloc("w2v", [128, 2, 64], BF16)
            peT = ar.alloc("peT", [128, 16], BF16)
            B_w2 = Buf()
            B_pe = Buf()

            P.emit(POOL, lambda e: e.memset(VV[:], 1.0), writes=B_VV)
            wt, wb = load_w("w_vtok", 280)

            def ev_vt(i, bk, bb):
                P.emit(ACT, lambda e: e.copy(out=VV[:, i, :, 0:64], in_=bk[:, 0:256].rearrange("p (a d) -> p a d", a=4)),
                       reads=[bb], writes=[B_VV[i]])
                P.emit(ACT, lambda e: e.activation(out=gates[:, i, :], in_=bk[:, 256:280], func=AF.Sigmoid),
                       reads=[bb], writes=[B_gates])
            proj_tm(wt, wb, 0, 280, ev_vt)

            kvcT = arx.alloc("kvcT", [128, 4, T], BF16)
            B_kvc = [Buf() for _ in range(4)]
            w1ring = Ring(arx, "w1r", [128, 16, 256], BF16, 2)
            wt, wb = load_w("w_kvc2", 512)

            def ev_kvc(tg, bk, bb, idx):
                P.emit(ACT, lambda e: e.copy(out=kvcT[0:64, idx, tg * 512:(tg + 1) * 512], in_=bk[0:64, :]),
                       reads=[bb], writes=[B_kvc[idx]])
                if tg == 0:
                    P.emit(ACT, lambda e: e.copy(out=kvcT[64:128, idx, 0:511], in_=bk[64:128, 1:512]),
                           reads=[bb], writes=[B_kvc[idx]])
                else:
                    P.emit(ACT, lambda e: e.copy(out=kvcT[64:128, idx, tg * 512 - 1:(tg + 1) * 512 - 1], in_=bk[64:128, :]),
                           reads=[bb], writes=[B_kvc[idx]])
            for idx in range(4):
                proj_fm(wt, wb, idx * 128, 128, lambda tg, bk, bb, idx=idx: ev_kvc(tg, bk, bb, idx))
            pool_cast_load(w2d[:], wd["cmp_k_w2d"].rearrange("(c p) m -> p c m", p=128), [B_w2])
            pool_cast_load(w2v[:], wd["cmp_v_w2"].rearrange("(c p) m -> p c m", p=128), [B_w2])
            for g in range(2):
                P.emit(DVE, lambda e, g=g: e.tensor_copy(out=Rg[g][:, 0:33], in_=ov1[:]), reads=[B_const], writes=[B_Rg[g]])
            for kvi, kv in enumerate(("k", "v")):
                w1t, w1b = w1ring.next()
                pool_cast_load(w1t[:], wd[f"cmp_{kv}_w1"].rearrange("(c p) h -> p c h", p=128), [w1b])
                pool_cast_load(peT[:], wd[f"cmp_{kv}_peT2"][:, :], [B_pe])
                sp_load(b1t[:], wd[f"cmp_{kv}_b1"][:, :], [B_b1t])
                bk, bb = nextbank()
                for hc in range(2):
                    for c in range(16):
                        P.emit(PE, lambda e, bk=bk, hc=hc, c=c, w1t=w1t: e.matmul(
                            bk[:, hc:hc + 1], lhsT=w1t[:, c, hc * 128:(hc + 1) * 128], rhs=peT[:, c:c + 1],
                            start=(c == 0), stop=(c == 15)), reads=[w1b, B_pe], writes=[bb])
                P.emit(DVE, lambda e, bk=bk: e.tensor_tensor(out=biasT[:], in0=bk[:, 0:2], in1=b1t[:], op=ALU.add),
                       reads=[bb, B_b1t], writes=[B_bias])
                for g in range(2):
                    idx = kvi * 2 + g
                    ht_, hb_ = hidT[g], B_hid[g]
                    for hc in range(2):
                        bk, bb = nextbank()
                        for c in range(16):
                            P.emit(PE, lambda e, bk=bk, hc=hc, c=c, idx=idx, w1t=w1t: e.matmul(
                                bk[:, 0:127], lhsT=w1t[:, c, hc * 128:(hc + 1) * 128],
                                rhs=kvcT[:, idx, 2 * c:2 * c + 16 * 126 + 1:16], start=(c == 0), stop=(c == 15)),
                                reads=[w1b, B_kvc[idx]], writes=[bb])
                        P.emit(ACT, lambda e, bk=bk, hc=hc, ht_=ht_: e.activation(
                            out=ht_[:, hc, 0:127], in_=bk[:, 0:127], func=AF.Silu, bias=biasT[:, hc:hc + 1], scale=1.0),
                            reads=[bb, B_bias], writes=[hb_])
                    bk, bb = nextbank()
                    if kv == "k":
                        for hc in range(2):
                            P.emit(PE, lambda e, bk=bk, hc=hc, ht_=ht_: e.matmul(
                                bk[:, 0:127], lhsT=w2d[:, hc, :], rhs=ht_[:, hc, 0:127], start=(hc == 0), stop=(hc == 1)),
                                reads=[B_w2, hb_], writes=[bb])
                        P.emit(ACT, lambda e, bk=bk, g=g: e.copy(out=kcmpT[g][:, 0:127], in_=bk[:, 0:127]),
                               reads=[bb], writes=[B_kcmp[g]])
                    else:
                        for hc in range(2):
                            P.emit(PE, lambda e, bk=bk, hc=hc, ht_=ht_: e.matmul(
                                bk[0:127, 0:64], lhsT=ht_[:, hc, 0:127], rhs=w2v[:, hc, :], start=(hc == 0), stop=(hc == 1)),
                                reads=[B_w2, hb_], writes=[bb])
                        P.emit(ACT, lambda e, bk=bk, g=g: e.copy(out=Rg[g][0:127, 33:97], in_=bk[0:127, 0:64]),
                               reads=[bb], writes=[B_Rg[g]])
            P.barrier()
            arx.reset(m_arx1)

            hnacc = arx.alloc("hnacc", [128, NT, 256], F32)
            nqT = arx.alloc("nqT", [128, 2, T], BF16)
            ksdT = arx.alloc("ksdT", [128, T], BF16)
            kwdT = arx.alloc("kwdT", [128, T], BF16)
            B_nq, B_ksd, B_kwd = Buf(), Buf(), Buf()
            maskS = [arx.alloc(f"maskS{j}", [128, 512], BF16) for j in range(16)]
            B_mask = [Buf() for _ in range(16)]
            ptring2 = Ring(ar, "pt2", [128, 512], BF16, 8)
            accs2 = Ring(ar, "accs2", [128, 260], F32, 4)

            def evac_branch(ab, abb, I, H, hh, br, first):
                av = ab[:, 0:260].rearrange("p (q c) -> p q c", c=65)
                d1, d1b = sm.next()
                P.emit(DVE, lambda e: e.reciprocal(out=d1[:, 0:4], in_=av[:, :, 64]), reads=[abb], writes=[d1b])
                P.emit(DVE, lambda e: e.tensor_tensor(out=d1[:, 0:4], in0=d1[:, 0:4], in1=gates[:, 4 * I:4 * I + 4, H * 3 + br], op=ALU.mult),
                       reads=[d1b, B_gates], writes=[d1b])
                for q in range(4):
                    i = 4 * I + q
                    dst = hnacc[:, i, hh * 64:(hh + 1) * 64]
                    if first:
                        P.emit(DVE, lambda e, q=q, dst=dst: e.tensor_scalar(out=dst, in0=av[:, q, 0:64], scalar1=d1[:, q:q + 1],
                                                                        scalar2=None, op0=ALU.mult),
                               reads=[abb, d1b], writes=[B_hn[i]])
                    else:
                        P.emit(DVE, lambda e, q=q, dst=dst: e.scalar_tensor_tensor(out=dst, in0=av[:, q, 0:64], scalar=d1[:, q:q + 1],
                                                                               in1=dst, op0=ALU.mult, op1=ALU.add),
                               reads=[abb, d1b, B_hn[i]], writes=[B_hn[i]])

            for g in range(2):
                for c in range(2):
                    wt, wb = load_w("w_nq", 128, c0=g * 256 + c * 128)
                    proj_fm(wt, wb, 0, 128, lambda tg, bk, bb, c=c: P.emit(ACT, lambda e: e.copy(
                        out=nqT[:, c, tg * 512:(tg + 1) * 512], in_=bk[:, :]), reads=[bb], writes=[B_nq]))
                wt, wb = load_w("w_ksd", 128, c0=g * 128)
                proj_fm(wt, wb, 0, 128, lambda tg, bk, bb: P.emit(ACT, lambda e: e.copy(
                    out=ksdT[:, tg * 512:(tg + 1) * 512], in_=bk[:, :]), reads=[bb], writes=[B_ksd]))
                wt, wb = load_w("w_kwd", 128, c0=g * 128)
                proj_fm(wt, wb, 0, 128, lambda tg, bk, bb: P.emit(ACT, lambda e: e.copy(
                    out=kwdT[:, tg * 512:(tg + 1) * 512], in_=bk[:, :]), reads=[bb], writes=[B_kwd]))
                if g == 1:
                    for i in range(8):
                        stg = hT[:, i, :].bitcast(F32)
                        P.emit(SP, lambda e, i=i, stg=stg: e.dma_start(out=stg, in_=xstv[:, i, :]), reads=[B_stash],
                               writes=B_hT, dma=True)
                P.emit(POOL, lambda e: e.memset(imp[:], 0.0), writes=B_imp)
                csteps = [(hh, I, {}) for hh in range(4) for I in range(4)]

                def c_A(hh, I, st, g=g):
                    c, half = hh // 2, hh % 2
                    r0 = half * 64
                    bs, bbs = nextbank()
                    P.emit(PE, lambda e: e.matmul(
                        bs[0:127, :], lhsT=kcmpT[g][r0:r0 + 64, 0:127], rhs=nqT[r0:r0 + 64, c, I * 512:(I + 1) * 512],
                        start=True, stop=True), reads=[B_kcmp[g], B_nq], writes=[bbs])
                    pt, pb = ptring.next()
                    P.emit(ACT, lambda e: e.activation(out=pt[0:127, :], in_=bs[0:127, :], func=AF.Exp, scale=SC_N),
                           reads=[bbs], writes=[pb])
                    P.emit(DVE, lambda e: e.tensor_tensor(out=pt[0:127, :], in0=pt[0:127, :],
                                                          in1=cmpmask[0:127, I * 512:(I + 1) * 512], op=ALU.mult),
                           reads=[pb, B_const], writes=[pb])
                    st["pt"] = (pt, pb)

                def c_C(hh, I, st, g=g):
                    H = 4 * g + hh
                    pt, pb = st["pt"]
                    ab, abb = nextbank()
                    for q in range(4):
                        P.emit(PE, lambda e, q=q: e.matmul(
                            ab[:, q * 97:(q + 1) * 97], lhsT=pt[0:127, q * 128:(q + 1) * 128], rhs=Rg[g][0:127, 0:97],
                            start=True, stop=True), reads=[pb, B_Rg[g]], writes=[abb])
                    av = ab[:, 0:388].rearrange("p (q c) -> p q c", c=97)
                    d1, d1b = sm.next()
                    d2, d2b = sm.next()

                    def ev():
                        P.emit(DVE, lambda e: e.tensor_scalar(out=d1[:, 0:4], in0=av[:, :, 32], scalar1=1e-30, scalar2=None, op0=ALU.max),
                               reads=[abb], writes=[d1b])
                        P.emit(DVE, lambda e: e.reciprocal(out=d1[:, 0:4], in_=d1[:, 0:4]), reads=[d1b], writes=[d1b])
                        P.emit(DVE, lambda e: e.tensor_tensor(
                            out=d2[:, 0:4], in0=d1[:, 0:4], in1=gates[:, 4 * I:4 * I + 4, H * 3 + 0], op=ALU.mult),
                            reads=[d1b, B_gates], writes=[d2b])
                        tmpv = junk[:, 0:128].rearrange("p (q c) -> p q c", c=32)
                        P.emit(DVE, lambda e: e.tensor_tensor(
                            out=tmpv, in0=av[:, :, 0:32], in1=d1[:, 0:4].unsqueeze(2).to_broadcast([128, 4, 32]), op=ALU.mult),
                            reads=[abb, d1b], writes=[B_junk])
                        P.emit(DVE, lambda e: e.tensor_tensor(
                            out=imp[:, 4 * I:4 * I + 4, :], in0=imp[:, 4 * I:4 * I + 4, :], in1=tmpv, op=ALU.add),
                            reads=[B_junk] + B_imp[4 * I:4 * I + 4], writes=B_imp[4 * I:4 * I + 4])
                        P.emit(DVE, lambda e: e.tensor_tensor(
                            out=hnacc[:, 4 * I:4 * I + 4, hh * 64:(hh + 1) * 64], in0=av[:, :, 33:97],
                            in1=d2[:, 0:4].unsqueeze(2).to_broadcast([128, 4, 64]), op=ALU.mult),
                            reads=[abb, d2b], writes=B_hn[4 * I:4 * I + 4])
                    defer(1, ev)

                run_pipeline(csteps, c_A, c_C)
                sel_fillers = []
                for I in range(4):
                    hold_I = {}
                    for q in range(4):
                        def f_sel(I=I, q=q, hold_I=hold_I):
                            i = 4 * I + q
                            if q == 0:
                                hold_I["bk"] = nextbank(hold=True)
                            bk, bb = hold_I["bk"]
                            bkb = bk[:, :].bitcast(BF16)
                            sc, scb = scr.next()
                            m8, m8b = sm.next()
                            wk, wkb = scr.next()
                            ns, nsb = nsr.next()
                            P.emit(DVE, lambda e: e.tensor_tensor(out=sc[:], in0=imp[:, i, :], in1=selvis[:, i, :], op=ALU.mult),
                                   reads=[B_imp[i], B_const], writes=[scb])
                            P.emit(DVE, lambda e: e.tensor_tensor(out=sc[:], in0=sc[:], in1=seladd[:, i, :], op=ALU.add),
                                   reads=[scb, B_const], writes=[scb])
                            P.emit(DVE, lambda e: e.max(out=m8[:, 0:8], in_=sc[:]), reads=[scb], writes=[m8b])
                            P.emit(DVE, lambda e: e.match_replace(out=wk[:], in_to_replace=m8[:, 0:8], in_values=sc[:],
                                                                  imm_value=-3.0e38), reads=[scb, m8b], writes=[wkb])
                            P.emit(DVE, lambda e: e.max(out=m8[:, 8:16], in_=wk[:]), reads=[wkb, m8b], writes=[m8b])
                            P.emit(DVE, lambda e: e.tensor_scalar(out=ns[:], in0=sc[:], scalar1=m8[:, 15:16], scalar2=None,
                                                                  op0=ALU.is_ge), reads=[scb, m8b], writes=[nsb])

                            def f_tr():
                                P.emit(PE, lambda e: e.transpose(bkb[0:32, q * 128:(q + 1) * 128], ns[:], ident_b[:]),
                                       reads=[nsb, B_const], writes=[bb])
                                if q == 3:
                                    def f_cp():
                                        P.emit(ACT, lambda e: e.copy(out=selT[0:32, I * 512:(I + 1) * 512], in_=bkb[0:32, 0:512]),
                                               reads=[bb], writes=[B_nst[I]])
                                        release(bb)
                                    defer(2, f_cp)
                            defer(2, f_tr)
                        sel_fillers.append(f_sel)
                steps = []
                for br in (2, 1):
                    if br not in DEBUG_BRANCHES:
                        continue
                    if br == 1:
                        steps.append(None)
                    for I in range(4):
                        for p in range(2):
                            st_I = {}
                            jlo = 0 if br == 1 else max(0, 4 * I - 4)
                            for j in range(jlo, 4 * I + 4):
                                steps.append((br, I, p, j, jlo, st_I))

                def n_A(br, I, p, j, jlo, st, g=g):
                    kTt, B_k = (ksdT, B_ksd) if br == 1 else (kwdT, B_kwd)
                    use_mask = (br == 1 and I >= 2)
                    qlo = max(0, j - 4 * I)
                    qhi = 3 if br == 1 else min(3, j - 4 * I + 4)
                    c0, c1 = qlo * 128, (qhi + 1) * 128
                    if use_mask and p == 0:
                        bk, bb = nextbank()
                        P.emit(PE, lambda e, bk=bk: e.matmul(
                            bk[:, c0:c1], lhsT=exm[0:32, j * 128:(j + 1) * 128], rhs=selT[0:32, I * 512 + c0:I * 512 + c1],
                            start=True, stop=True), reads=[B_const, B_nst[I]], writes=[bb])
                        P.emit(DVE, lambda e, bk=bk: e.tensor_copy(out=maskS[j][:, c0:c1], in_=bk[:, c0:c1]),
                               reads=[bb], writes=[B_mask[j]])
                        if j - 4 * I >= 0:
                            P.emit(DVE, lambda e: e.tensor_tensor(
                                out=maskS[j][:, c0:c0 + 128], in0=maskS[j][:, c0:c0 + 128], in1=trile[:], op=ALU.mult),
                                reads=[B_mask[j], B_const], writes=[B_mask[j]])
                    bsl = []
                    for half in range(2):
                        r0 = half * 64
                        bs, bbs = nextbank()
                        P.emit(PE, lambda e, bs=bs, r0=r0: e.matmul(
                            bs[:, c0:c1], lhsT=kTt[r0:r0 + 64, j * 128:(j + 1) * 128], rhs=nqT[r0:r0 + 64, p, I * 512 + c0:I * 512 + c1],
                            start=True, stop=True), reads=[B_k, B_nq], writes=[bbs])
                        bsl.append((bs, bbs))
                    ptl = []
                    for half in range(2):
                        bs, bbs = bsl[half]
                        pt, pb = ptring2.next()
                        P.emit(ACT, lambda e, pt=pt, bs=bs: e.activation(out=pt[:, c0:c1], in_=bs[:, c0:c1], func=AF.Exp, scale=SC_N),
                               reads=[bbs], writes=[pb])
                        if use_mask:
                            P.emit(DVE, lambda e, pt=pt: e.tensor_tensor(
                                out=pt[:, c0:c1], in0=pt[:, c0:c1], in1=maskS[j][:, c0:c1], op=ALU.mult),
                                reads=[pb, B_mask[j]], writes=[pb])
                        else:
                            for q in range(qlo, qhi + 1):
                                r = j - (4 * I + q)
                                msk = trile if r == 0 else (trigt if (r == -4 and br == 2) else None)
                                if msk is not None:
                                    P.emit(DVE, lambda e, pt=pt, q=q, msk=msk: e.tensor_tensor(
                                        out=pt[:, q * 128:(q + 1) * 128], in0=pt[:, q * 128:(q + 1) * 128], in1=msk[:], op=ALU.mult),
                                        reads=[pb, B_const], writes=[pb])
                        ptl.append((pt, pb))
                    st[("pt", j)] = (ptl, qlo, qhi)

                def n_C(br, I, p, j, jlo, st, g=g):
                    vidx = 2 * g + (0 if br == 1 else 1)
                    ptl, qlo, qhi = st.pop(("pt", j))
                    if j == jlo:
                        st["ab"] = [nextbank(hold=True), nextbank(hold=True)]
                        st["fresh"] = [True, True]
                    for half in range(2):
                        pt, pb = ptl[half]
                        ab, abb = st["ab"][half]
                        for q in range(qlo, qhi + 1):
                            i = 4 * I + q
                            first = st["fresh"][half]
                            st["fresh"][half] = False
                            P.emit(PE, lambda e, ab=ab, q=q, pt=pt, i=i, first=first: e.matmul(
                                ab[:, q * 65:(q + 1) * 65], lhsT=pt[:, q * 128:(q + 1) * 128], rhs=VV[:, j, vidx, 0:65],
                                start=first, stop=(j == i), skip_group_check=True), reads=[pb, B_VV[j]], writes=[abb])
                    if j == 4 * I + 3:
                        for half in range(2):
                            hh = 2 * p + half
                            ab, abb = st["ab"][half]
                            cs, csb = accs2.next()
                            P.emit(DVE, lambda e, ab=ab, cs=cs: e.tensor_copy(out=cs[:, 0:260], in_=ab[:, 0:260]),
                                   reads=[abb], writes=[csb])
                            release(abb)
                            defer(1, lambda cs=cs, csb=csb, hh=hh: evac_branch(cs, csb, I, 4 * g + hh, hh, br, False))

                if None in steps:
                    k = steps.index(None)
                    run_pipeline(steps[:k], n_A, n_C, fillers=sel_fillers)
                    run_pipeline(steps[k + 1:], n_A, n_C)
                else:
                    run_pipeline(steps, n_A, n_C, fillers=sel_fillers)
                for i in range(NT):
                    P.emit(ACT, lambda e, i=i, g=g: e.copy(out=mixcat[:, i, 512 + g * 256:512 + (g + 1) * 256], in_=hnacc[:, i, :]),
                           reads=[B_hn[i]], writes=[B_mix[i]])
            dump("d_kcmp0", kcmpT[0][:], [B_kcmp[0]])
            dump("d_Rg0", Rg[0][:], [B_Rg[0]])
            dump("d_imp", imp[:].rearrange("p i b -> p (i b)"), B_imp)
            dump("d_nst", selT[:], B_nst)
            dump("d_gates", gates[:].rearrange("p i b -> p (i b)"), [B_gates])
            dump("d_mix", mixcat[:].rearrange("p i d -> p (i d)"), B_mix)
            P.barrier()
            ar.reset(m_ar1)

            for i in range(8, NT):
                if i % 2 == 0:
                    sp_load(xs[:, i, :], xstv[:, i, :], [B_x[i]], reads=[B_stash])
                else:
                    P.emit(POOL, lambda e, i=i: e.dma_start(out=xs[:, i, :], in_=xstv[:, i, :]), writes=[B_x[i]],
                           reads=[B_stash], dma=True)
            wout = ar.alloc("wout", [128, 8, D], BF16)
            B_wout = Buf()
            wov = wd["w_out"].rearrange("(k p) m -> p k m", p=128)
            pool_cast_load(wout[:, :, 0:512], wov[:, :, 0:512], [B_wout])
            pool_cast_load(wout[:, :, 512:1024], wov[:, :, 512:1024], [B_wout])
            mcr = Ring(ar, "mcT", [128, 8, 128], BF16, 2)
            for i in range(NT):
                bk, bb = nextbank()
                bkb = bk[:, :].bitcast(BF16)
                for k in range(8):
                    P.emit(PE, lambda e, k=k, i=i, bkb=bkb: e.transpose(bkb[:, k * 128:(k + 1) * 128], mixcat[:, i, k * 128:(k + 1) * 128], ident_b[:]),
                           reads=[B_mix[i], B_const], writes=[bb])
                mc, mcb = mcr.next()
                P.emit(ACT, lambda e, mc=mc, bkb=bkb: e.copy(out=mc[:].rearrange("p k t -> p (k t)"), in_=bkb), reads=[bb], writes=[mcb])
                for half in range(2):
                    by, bby = nextbank()
                    for k in range(8):
                        P.emit(PE, lambda e, k=k, half=half, by=by, mc=mc: e.matmul(
                            by[:, :], lhsT=mc[:, k, :], rhs=wout[:, k, half * 512:(half + 1) * 512], start=(k == 0), stop=(k == 7)),
                            reads=[mcb, B_wout], writes=[bby])
                    xin = hT[:, i, :].bitcast(F32)[:, half * 512:(half + 1) * 512] if i < 8 else xs[:, i, half * 512:(half + 1) * 512]
                    P.emit(DVE, lambda e, i=i, half=half, by=by, xin=xin: e.tensor_tensor(
                        out=xs[:, i, half * 512:(half + 1) * 512], in0=by[:, :], in1=xin, op=ALU.add),
                        reads=[bby, B_x[i]] + (B_hT if i < 8 else []), writes=[B_x[i]])
            load_gain("ffn2_norm")
            norm_to_hT(ar)
            ar.reset(m_ar0)
            dump("d_x2", xs[:].rearrange("p i d -> p (i d)"), B_x)
            P.barrier()

        for s in range(nseq):
            xv = x_d[s].rearrange("(i p) d -> p i d", p=128)
            for i in range(NT):
                sp_load(xs[:, i, :], xv[:, i, :], [B_x[i]])
            load_gain("ffn1_norm")
            norm_begin()
            ffn(wd["ffn1_w1"], wd["ffn1_w3"], wd["ffn1_w2"], ar, pre_tg=lambda tg: norm_group(4 * tg),
                post_tile=(stash_tile if do_mixer else None))
            if s == 0:
                dump("d_x1", xs[:].rearrange("p i d -> p (i d)"), B_x)
            P.barrier()
            if do_mixer:
                ar.reset(phase_mark)
                mixer(s)
            if not do_mixer:
                load_gain("ffn2_norm")
                norm_to_hT(ar)
            ffn(wd["ffn2_w1"], wd["ffn2_w3"], wd["ffn2_w2"], ar)
            load_gain("final_norm")
            junk = norm_junk
            ot = ffn_bufs["ot"]
            P.emit(DVE, lambda e: e.memset(ss[:], 0.0), writes=[B_ss])
            ov = out_d[s].rearrange("(i p) d -> p i d", p=128)
            for i in range(NT):
                jt, jb = junk.next()
                P.emit(ACT, lambda e, i=i, jt=jt: e.activation(out=jt[:], in_=xs[:, i, :], func=AF.Square,
                                                               accum_out=ss[:, i:i + 1]),
                       reads=[B_x[i]], writes=[jb, B_ss])
            rms_rstd_grp(1.0 / D, 0, NT)
            for i in range(NT):
                o_t, o_b = ot.next()
                P.emit(DVE, lambda e, i=i, o_t=o_t: e.scalar_tensor_tensor(out=o_t[:], in0=xs[:, i, :], scalar=rstd[:, i:i + 1],
                                                                           in1=gbc[:], op0=ALU.mult, op1=ALU.mult),
                       reads=[B_x[i], B_rstd, B_gbc], writes=[o_b])
                P.final_ops.append(P.emit(SP, lambda e, i=i, o_t=o_t, ov=ov: e.dma_start(out=ov[:, i, :], in_=o_t[:]),
                                          reads=[o_b], dma=True))

        sems = {}
        for nm in P.sem_names():
            sems[nm] = es.enter_context(nc.semaphore("_".join(str(v) for v in nm)))
        P.build(sems)
        with nc.Block() as block:
            @block.tensor
            def _(e):
                P.replay(PE, e)

            @block.scalar
            def _(e):
                P.replay(ACT, e)

            @block.vector
            def _(e):
                P.replay(DVE, e)

            @block.gpsimd
            def _(e):
                P.replay(POOL, e)

            @block.sync
            def _(e):
                P.replay(SP, e)
    return nc


def _prep_weights(inp):
    f = lambda a: np.ascontiguousarray(np.asarray(a, dtype=np.float32))
    w = {}
    for ff in ("ffn1", "ffn2"):
        for nm in ("w1", "w3", "w2"):
            w[f"{ff}_{nm}"] = f(inp[f"{ff}_{nm}"][0])
        w[f"{ff}_norm"] = f(inp[f"{ff}_norm"][0][None, :])
    w["mix_norm"] = f(inp["mix_norm"][0][None, :])
    w["final_norm"] = f(np.asarray(inp["final_norm"])[None, :])
    wi = np.asarray(inp["w_in"][0], dtype=np.float32)
    sizes = [512] * 4 + [4] * 2 + [512] + [128] * 6 + [24]
    offs = np.concatenate([[0], np.cumsum(sizes)])
    parts = [wi[:, offs[i]:offs[i + 1]] for i in range(len(sizes))]
    mq, mk, mv, mo, mi, mf, nq, kc, vc, ks, vs, kw, vw, ng = parts
    w["w_gi"] = f(mi)
    w["w_gf"] = f(mf)
    w["w_mq"] = f(mq)
    w["w_mk"] = f(mk)
    w["w_mv"] = f(mv)
    w["w_mo"] = f(mo)
    w["w_nq"] = f(nq)
    w["w_ksd"] = f(np.concatenate([ks[:, 0:64], ks[:, 0:64], ks[:, 64:128], ks[:, 64:128]], axis=1))
    w["w_kwd"] = f(np.concatenate([kw[:, 0:64], kw[:, 0:64], kw[:, 64:128], kw[:, 64:128]], axis=1))
    kvc = np.concatenate([kc, vc], axis=1)
    w["w_kvc2"] = f(np.concatenate([np.concatenate([kvc[:, i * 64:(i + 1) * 64]] * 2, axis=1) for i in range(4)], axis=1))
    w["w_vtok"] = f(np.concatenate([vs[:, 0:64], vw[:, 0:64], vs[:, 64:128], vw[:, 64:128], ng], axis=1))
    w["conv_wT"] = f(np.asarray(inp["conv_w"][0]).T)
    w["ml_b_i"] = f(np.asarray(inp["ml_b_i"][0])[:, None])
    w["ml_b_f"] = f(np.asarray(inp["ml_b_f"][0])[:, None])
    w["ml_gn"] = f(np.asarray(inp["ml_gn"][0])[None, :])
    for kv in ("k", "v"):
        w[f"cmp_{kv}_peT2"] = f(np.asarray(inp[f"cmp_{kv}_pe"][0]).reshape(16, 128).T)
        w[f"cmp_{kv}_w1"] = f(inp[f"cmp_{kv}_w1"][0])
        w[f"cmp_{kv}_b1"] = f(np.asarray(inp[f"cmp_{kv}_b1"][0]).reshape(2, 128).T)
    kw2 = np.asarray(inp["cmp_k_w2"][0], dtype=np.float32)
    w["cmp_k_w2d"] = f(np.concatenate([kw2, kw2], axis=1))
    w["cmp_v_w2"] = f(inp["cmp_v_w2"][0])
    w["w_out"] = f(inp["w_out"][0])
    w.update(_consts())
    return w


def kernel(**inputs):
    n_cores = 8
    x = np.asarray(inputs["x"], dtype=np.float32)
    nseq = x.shape[0] // n_cores
    w = _prep_weights(inputs)
    nc = build_program(nseq)
    in_maps = []
    for c in range(n_cores):
        m = dict(w)
        m["x"] = np.ascontiguousarray(x[c * nseq:(c + 1) * nseq])
        in_maps.append(m)
    res = run_bass_kernel_spmd(nc, in_maps, core_ids=list(range(n_cores)))
    return np.concatenate([r["out"] for r in res.results], axis=0).astype(np.float32)
```

```python
import numpy as np
from contextlib import ExitStack
import concourse.bass as bass
import concourse.mybir as mybir
from concourse.bass_utils import run_bass_kernel_spmd

F32 = mybir.dt.float32
BF16 = mybir.dt.bfloat16
AF = mybir.ActivationFunctionType
ALU = mybir.AluOpType

PE, ACT, DVE, POOL, SP = "tensor", "scalar", "vector", "gpsimd", "sync"
ENGINES = [PE, ACT, DVE, POOL, SP]
NDMA_SEM = 8

T = 2048
D = 1024
DFF = 2816
NT = 16
EPS = 1e-6
NEGBIG = -30000.0
DEBUG_BRANCHES = (1, 2)


class Buf:
    __slots__ = ("name", "w", "r")

    def __init__(self, name=""):
        self.name = name
        self.w = None
        self.r = []


class Op:
    __slots__ = ("eng", "idx", "fn", "deps", "is_dma", "sem", "val", "waited", "dma_prev")

    def __init__(self, eng, idx, fn, is_dma):
        self.eng = eng
        self.idx = idx
        self.fn = fn
        self.deps = {}
        self.is_dma = is_dma
        self.sem = None
        self.val = None
        self.waited = False
        self.dma_prev = None


class Prog:
    def __init__(self, nc):
        self.nc = nc
        self.ops = {e: [] for e in ENGINES}
        self.dma_count = {e: 0 for e in ENGINES}
        self.dma_last = {}
        self.final_ops = []

    def _add_dep(self, op, d):
        if d is None or d is op:
            return
        if d.is_dma:
            op.deps[("dma", id(d))] = d
            return
        if d.eng == PE and op.eng == PE and not op.is_dma:
            return
        k = ("eng", d.eng)
        cur = op.deps.get(k)
        if cur is None or cur.idx < d.idx:
            op.deps[k] = d

    def emit(self, eng, fn, reads=(), writes=(), dma=False, extra_deps=()):
        op = Op(eng, len(self.ops[eng]), fn, dma)
        for b in reads:
            self._add_dep(op, b.w)
        for b in writes:
            self._add_dep(op, b.w)
            for r in b.r:
                self._add_dep(op, r)
        for d in extra_deps:
            self._add_dep(op, d)
        for b in reads:
            b.r.append(op)
        for b in writes:
            b.w = op
            b.r = []
        if dma:
            n = self.dma_count[eng]
            self.dma_count[eng] = n + 1
            slot = n % NDMA_SEM
            prev = self.dma_last.get((eng, slot))
            op.dma_prev = prev
            self.dma_last[(eng, slot)] = op
            op.sem = ("dma", eng, slot)
            op.val = 16 * (n // NDMA_SEM + 1)
            if prev is not None:
                prev.waited = True
        for d in op.deps.values():
            d.waited = True
        self.ops[eng].append(op)
        return op

    def last_real(self, eng):
        for op in reversed(self.ops[eng]):
            if op.fn is not None:
                return op
        return None

    def barrier(self):
        lasts = [self.last_real(e) for e in ENGINES]
        dmas = list(self.dma_last.values())
        for e in ENGINES:
            self.emit(e, None, extra_deps=[l for l in lasts if l is not None] + dmas)

    def build(self, sems):
        for o in self.final_ops:
            o.waited = True
        for e in ENGINES:
            c = 0
            for op in self.ops[e]:
                if op.is_dma or op.fn is None:
                    continue
                if op.waited:
                    c += 1
                    op.sem = ("eng", e)
                    op.val = c
        self.sem_handles = sems

    def replay(self, engname, eng):
        sems = self.sem_handles
        cur_wait = {}

        def wait(d):
            if cur_wait.get(d.sem, 0) >= d.val:
                return
            cur_wait[d.sem] = d.val
            eng.wait_ge(sems[d.sem], d.val)

        for op in self.ops[engname]:
            if op.is_dma and op.dma_prev is not None:
                wait(op.dma_prev)
            for d in op.deps.values():
                wait(d)
            if op.fn is None:
                continue
            ins = op.fn(eng)
            if op.is_dma:
                ins.then_inc(sems[op.sem], 16)
            elif op.waited:
                ins.then_inc(sems[op.sem], 1)
        if engname == SP:
            for o in self.final_ops:
                wait(o)

    def sem_names(self):
        names = [("eng", e) for e in ENGINES]
        for e in ENGINES:
            if self.dma_count[e]:
                names += [("dma", e, s) for s in range(NDMA_SEM)]
        return names


_DTSIZE = {F32: 4, BF16: 2}


class Arena:
    def __init__(self, nc, base, cap):
        self.nc = nc
        self.base = base
        self.off = base
        self.cap = cap
        self.n = 0

    def alloc(self, name, shape, dt=F32):
        size = int(np.prod(shape[1:])) * _DTSIZE[dt]
        size = (size + 63) // 64 * 64
        assert self.off + size <= self.cap, f"SBUF arena overflow at {name}: {self.off + size} > {self.cap}"
        self.n += 1
        t = self.nc.alloc_sbuf_tensor_at(f"{name}_{self.n}_{self.off}", list(shape), dt, offset=self.off)
        self.off += size
        return t

    def mark(self):
        return self.off

    def reset(self, m):
        self.off = m


class Ring:
    def __init__(self, arena, name, shape, dt, n):
        self.items = [(arena.alloc(f"{name}{i}", shape, dt), Buf(f"{name}{i}")) for i in range(n)]
        self.i = 0

    def next(self):
        it = self.items[self.i % len(self.items)]
        self.i += 1
        return it


def _consts():
    c = {}
    s = np.arange(128)[:, None]
    t = np.arange(128)[None, :]
    c["c_ident"] = np.eye(128, dtype=np.float32)
    c["c_trile"] = (s <= t).astype(np.float32)
    c["c_trigt"] = (s > t).astype(np.float32)
    n = np.arange(128)[:, None]
    tt = np.arange(T)[None, :]
    cm = ((16 * n + 31) <= tt).astype(np.float32)
    cm[127] = 0.0
    c["c_cmpmask"] = cm
    b = np.arange(32)[:, None]
    col = np.arange(16 * 128)[None, :]
    c["c_ex"] = (b == (2 * (col // 128) + (col % 128) // 64)).astype(np.float32)
    c0 = np.arange(128)[:, None] * 16
    s0 = np.arange(32)[None, :] * 64
    ov = np.clip(np.minimum(c0 + 32, s0 + 64) - np.maximum(c0, s0), 0, None).astype(np.float32) / 32.0
    ov1 = np.concatenate([ov, np.ones((128, 1), np.float32)], axis=1)
    ov1[127] = 0.0
    c["c_ov1"] = ov1
    tq = np.arange(T)[:, None]
    blk = np.arange(32)[None, :]
    cur = tq // 64
    visible = blk * 64 <= tq
    forced = (blk == 0) | (blk == cur) | (blk == cur - 1)
    vis01 = (visible & ~forced).astype(np.float32)
    fval = np.where(blk == 0, 3e30, np.where(blk == cur, 2e30, 1e30))
    add = np.where(forced, fval, np.where(visible, 0.0, -1e30 - blk * 1e28)).astype(np.float32)
    c["c_selvis"] = vis01
    c["c_seladd"] = add
    r = np.arange(4)[:, None]
    cc = np.arange(4 * 128)[None, :]
    c["c_sel4"] = (r == cc // 128).astype(np.float32)
    c["c_ident4"] = np.eye(4, dtype=np.float32)
    return c


def build_program(nseq, debug=(), do_mixer=True):
    nc = bass.Bass("TRN2", target_bir_lowering=False)
    P = Prog(nc)

    def din(name, shape):
        return nc.dram_tensor(name, list(shape), F32, kind="ExternalInput").ap()

    x_d = din("x", [nseq, T, D])
    out_d = nc.dram_tensor("out", [nseq, T, D], F32, kind="ExternalOutput").ap()
    xst_d = nc.dram_tensor("xstash", [T, D], F32, kind="Internal").ap()
    wd = {}
    for ff in ("ffn1", "ffn2"):
        wd[ff + "_w1"] = din(ff + "_w1", [D, DFF])
        wd[ff + "_w3"] = din(ff + "_w3", [D, DFF])
        wd[ff + "_w2"] = din(ff + "_w2", [DFF, D])
    for nm in ("ffn1_norm", "mix_norm", "ffn2_norm", "final_norm"):
        wd[nm] = din(nm, [1, D])
    wd["w_gi"] = din("w_gi", [D, 4])
    wd["w_gf"] = din("w_gf", [D, 4])
    wd["w_mq"] = din("w_mq", [D, 512])
    wd["w_mk"] = din("w_mk", [D, 512])
    wd["w_mv"] = din("w_mv", [D, 512])
    wd["w_mo"] = din("w_mo", [D, 512])
    wd["w_nq"] = din("w_nq", [D, 512])
    wd["w_ksd"] = din("w_ksd", [D, 256])
    wd["w_kwd"] = din("w_kwd", [D, 256])
    wd["w_kvc2"] = din("w_kvc2", [D, 512])
    wd["w_vtok"] = din("w_vtok", [D, 280])
    wd["conv_wT"] = din("conv_wT", [D, 4])
    wd["ml_b_i"] = din("ml_b_i", [4, 1])
    wd["ml_b_f"] = din("ml_b_f", [4, 1])
    wd["ml_gn"] = din("ml_gn", [1, 512])
    for kv in ("k", "v"):
        wd[f"cmp_{kv}_peT2"] = din(f"cmp_{kv}_peT2", [128, 16])
        wd[f"cmp_{kv}_w1"] = din(f"cmp_{kv}_w1", [2048, 256])
        wd[f"cmp_{kv}_b1"] = din(f"cmp_{kv}_b1", [128, 2])
    wd["cmp_k_w2d"] = din("cmp_k_w2d", [256, 128])
    wd["cmp_v_w2"] = din("cmp_v_w2", [256, 64])
    wd["w_out"] = din("w_out", [D, D])
    cd = {}
    for nm, arr in _consts().items():
        cd[nm] = din(nm, arr.shape)
    dbg = {}
    for nm, shape in debug:
        dbg[nm] = nc.dram_tensor(nm, list(shape), F32, kind="ExternalOutput").ap()

    es = ExitStack()
    with es:
        SB_TOTAL = 224 * 1024
        base = (SB_TOTAL - int(nc.sbuf_bytes_remaining) + 127) // 64 * 64
        ar = Arena(nc, base, SB_TOTAL - 64)
        banks = [(es.enter_context(nc.psum_tensor(f"bank{i}", [128, 512], F32)), Buf(f"bank{i}")) for i in range(8)]
        bank_i = [0]
        held = set()

        def nextbank(hold=False):
            while True:
                idx = bank_i[0] % 8
                bank_i[0] += 1
                if idx not in held:
                    break
            if hold:
                held.add(idx)
            return banks[idx]

        def release(bb):
            for idx, (_, b) in enumerate(banks):
                if b is bb:
                    held.discard(idx)

        ident_b = ar.alloc("ident_b", [128, 128], BF16)
        trile = ar.alloc("trile", [128, 128], BF16)
        trigt = ar.alloc("trigt", [128, 128], BF16)
        cmpmask = ar.alloc("cmpmask", [128, T], BF16)
        exm = ar.alloc("exm", [32, 16 * 128], BF16)
        ov1 = ar.alloc("ov1", [128, 33], BF16)
        selvis = ar.alloc("selvis", [128, NT, 32], F32)
        seladd = ar.alloc("seladd", [128, NT, 32], F32)
        sel4 = ar.alloc("sel4", [4, 512], F32)
        ident4 = ar.alloc("ident4", [4, 4], F32)
        gbc = ar.alloc("gbc", [128, D], F32)
        gnbc = ar.alloc("gnbc", [128, 512], F32)
        convw = ar.alloc("convw", [128, 8, 4], F32)
        b_i = ar.alloc("b_i", [4, 1], F32)
        b_f = ar.alloc("b_f", [4, 1], F32)
        nb_f = ar.alloc("nb_f", [4, 1], F32)
        epsb = ar.alloc("epsb", [128, 1], F32)
        ss = ar.alloc("ss", [128, NT], F32)
        rstd = ar.alloc("rstd", [128, NT], F32)
        B_const = Buf("const")
        B_gbc = Buf("gbc")
        B_ss = Buf("ss")
        B_rstd = Buf("rstd")
        xs_off = ar.mark()
        xs = ar.alloc("xs", [128, NT, D], F32)
        B_x = [Buf(f"x{i}") for i in range(NT)]
        hT = ar.alloc("hT", [128, 8, T], BF16)
        B_hT = [Buf(f"hT{i}") for i in range(4)]
        B_stash = Buf("stash")
        norm_junk = Ring(ar, "nj", [128, D], BF16, 1)
        norm_hn = Ring(ar, "nh", [128, D], BF16, 2)
        phase_mark = ar.mark()

        def pool_cast_load(dst, src, writes):
            return P.emit(POOL, lambda e: e.dma_start(out=dst, in_=src), writes=writes, dma=True)

        def sp_load(dst, src, writes, reads=()):
            return P.emit(SP, lambda e: e.dma_start(out=dst, in_=src), writes=writes, reads=reads, dma=True)

        pool_cast_load(ident_b[:], cd["c_ident"][:, :], [B_const])
        P.emit(DVE, lambda e: e.memset(epsb[:], EPS), writes=[B_const])
        consts_loaded = [False]

        def load_consts_late():
            if consts_loaded[0]:
                return
            consts_loaded[0] = True
            pool_cast_load(trile[:], cd["c_trile"][:, :], [B_const])
            pool_cast_load(trigt[:], cd["c_trigt"][:, :], [B_const])
            pool_cast_load(cmpmask[:], cd["c_cmpmask"][:, :], [B_const])
            pool_cast_load(exm[:], cd["c_ex"][:, :], [B_const])
            pool_cast_load(ov1[:], cd["c_ov1"][:, :], [B_const])
            sp_load(selvis[:], cd["c_selvis"].rearrange("(i p) b -> p i b", p=128), [B_const])
            sp_load(seladd[:], cd["c_seladd"].rearrange("(i p) b -> p i b", p=128), [B_const])
            sp_load(sel4[:], cd["c_sel4"][:, :], [B_const])
            sp_load(ident4[:], cd["c_ident4"][:, :], [B_const])
            sp_load(gnbc[:], bass.AP(wd["ml_gn"].tensor, 0, [[0, 128], [1, 512]]), [B_const])
            sp_load(convw[:], wd["conv_wT"].rearrange("(k p) j -> p k j", p=128), [B_const])
            sp_load(b_i[:], wd["ml_b_i"][:, :], [B_const])
            sp_load(b_f[:], wd["ml_b_f"][:, :], [B_const])
            P.emit(DVE, lambda e: e.tensor_scalar(out=nb_f[:], in0=b_f[:], scalar1=-1.0, scalar2=None, op0=ALU.mult),
                   reads=[B_const], writes=[B_const])

        def load_gain(name):
            sp_load(gbc[:], bass.AP(wd[name].tensor, 0, [[0, 128], [1, D]]), [B_gbc])

        def rms_rstd_grp(width_scale, i0, n):
            P.emit(ACT, lambda e: e.activation(out=rstd[:, i0:i0 + n], in_=ss[:, i0:i0 + n], func=AF.Sqrt, bias=epsb[:, 0:1], scale=width_scale),
                   reads=[B_ss, B_const], writes=[B_rstd])
            P.emit(DVE, lambda e: e.reciprocal(out=rstd[:, i0:i0 + n], in_=rstd[:, i0:i0 + n]), reads=[B_rstd], writes=[B_rstd])

        def norm_to_hT(tmp_arena):
            norm_begin()
            for i0 in range(0, NT, 4):
                norm_group(i0)

        def norm_begin():
            P.emit(DVE, lambda e: e.memset(ss[:], 0.0), writes=[B_ss])

        def norm_group(i0):
            junk, hn = norm_junk, norm_hn
            if True:
                for i in range(i0, i0 + 4):
                    jt, jb = junk.next()
                    P.emit(ACT, lambda e, i=i, jt=jt: e.activation(out=jt[:], in_=xs[:, i, :], func=AF.Square,
                                                                   accum_out=ss[:, i:i + 1]),
                           reads=[B_x[i]], writes=[jb, B_ss])
                rms_rstd_grp(1.0 / D, i0, 4)
                for i in range(i0, i0 + 4):
                    ht, hb = hn.next()
                    P.emit(DVE, lambda e, i=i, ht=ht: e.scalar_tensor_tensor(out=ht[:], in0=xs[:, i, :], scalar=rstd[:, i:i + 1],
                                                                             in1=gbc[:], op0=ALU.mult, op1=ALU.mult),
                           reads=[B_x[i], B_rstd, B_gbc], writes=[hb])
                    bk, bb = nextbank()
                    bkb = bk[:, :].bitcast(BF16)
                    for k in range(8):
                        P.emit(PE, lambda e, k=k, ht=ht, bkb=bkb: e.transpose(bkb[:, k * 128:(k + 1) * 128], ht[:, k * 128:(k + 1) * 128], ident_b[:]),
                               reads=[hb, B_const], writes=[bb])
                    P.emit(ACT, lambda e, i=i, bkb=bkb: e.copy(out=hT[:, :, i * 128:(i + 1) * 128],
                                                               in_=bkb.rearrange("p (k t) -> p k t", k=8)),
                           reads=[bb], writes=[B_hT[i // 4]])

        ffn_bufs = {}

        def ffn(w1_d, w3_d, w2_d, tmp_arena, pre_tg=None, post_tile=None):
            groups = [(f0, min(512, DFF - f0)) for f0 in range(0, DFF, 512)]
            if not ffn_bufs:
                tmp_arena.reset(phase_mark)
                ffn_bufs["w1r"] = Ring(tmp_arena, "w1g", [128, 8, 512], BF16, 2)
                ffn_bufs["w3r"] = Ring(tmp_arena, "w3g", [128, 8, 512], BF16, 2)
                ffn_bufs["w2r"] = Ring(tmp_arena, "w2g", [128, 4, D], BF16, 2)
                ffn_bufs["aT"] = tmp_arena.alloc("aT", [128, 4, T], BF16)
                ffn_bufs["B_aT"] = [[Buf() for _ in range(4)] for _ in range(4)]
                ffn_bufs["sil"] = Ring(tmp_arena, "sil", [128, 512], F32, 2)
                ffn_bufs["ot"] = Ring(tmp_arena, "fo", [128, D], F32, 2)
            w1r, w3r, w2r = ffn_bufs["w1r"], ffn_bufs["w3r"], ffn_bufs["w2r"]
            aT, B_aT, sil = ffn_bufs["aT"], ffn_bufs["B_aT"], ffn_bufs["sil"]
            w1v = w1_d.rearrange("(k p) f -> p k f", p=128)
            w3v = w3_d.rearrange("(k p) f -> p k f", p=128)
            w2v = w2_d.rearrange("(c p) d -> p c d", p=128)

            def load_group(g):
                f0, fw = groups[g]
                nfc = fw // 128
                t1, b1 = w1r.next()
                t3, b3 = w3r.next()
                t2, b2 = w2r.next()
                pool_cast_load(t1[:, :, 0:fw], w1v[:, :, f0:f0 + fw], [b1])
                pool_cast_load(t3[:, :, 0:fw], w3v[:, :, f0:f0 + fw], [b3])
                pool_cast_load(t2[:, 0:nfc, :], w2v[:, f0 // 128:f0 // 128 + nfc, :], [b2])
                return (t1, b1, t3, b3, t2, b2)

            cur = load_group(0)
            for g, (f0, fw) in enumerate(groups):
                nxt = load_group(g + 1) if g + 1 < len(groups) else None
                if g == len(groups) - 1:
                    load_consts_late()
                t1, b1, t3, b3, t2, b2 = cur
                nfc = fw // 128
                order = [(fc, tg) for fc in range(nfc) for tg in range(4)]
                if g == 0 and pre_tg is not None:
                    order = [(fc, tg) for tg in range(4) for fc in range(nfc)]
                for fc, tg in order:
                    if g == 0 and pre_tg is not None and fc == 0:
                        pre_tg(tg)
                    if True:
                        bu, bbu = nextbank()
                        bv, bbv = nextbank()
                        for k in range(8):
                            P.emit(PE, lambda e, k=k, fc=fc, tg=tg, bu=bu, t1=t1: e.matmul(
                                bu[:, :], lhsT=t1[:, k, fc * 128:(fc + 1) * 128], rhs=hT[:, k, tg * 512:(tg + 1) * 512],
                                start=(k == 0), stop=(k == 7)), reads=[b1, B_hT[tg]], writes=[bbu])
                        for k in range(8):
                            P.emit(PE, lambda e, k=k, fc=fc, tg=tg, bv=bv, t3=t3: e.matmul(
                                bv[:, :], lhsT=t3[:, k, fc * 128:(fc + 1) * 128], rhs=hT[:, k, tg * 512:(tg + 1) * 512],
                                start=(k == 0), stop=(k == 7)), reads=[b3, B_hT[tg]], writes=[bbv])
                        st, sbf = sil.next()
                        P.emit(ACT, lambda e, bu=bu, st=st: e.activation(out=st[:], in_=bu[:, :], func=AF.Silu),
                               reads=[bbu], writes=[sbf])
                        P.emit(DVE, lambda e, fc=fc, tg=tg, st=st, bv=bv: e.tensor_tensor(
                            out=aT[:, fc, tg * 512:(tg + 1) * 512], in0=st[:], in1=bv[:, :], op=ALU.mult),
                            reads=[sbf, bbv], writes=[B_aT[fc][tg]])
                for i in range(NT):
                    by0, bby0 = nextbank()
                    by1, bby1 = nextbank()
                    for half, (by, bby) in enumerate(((by0, bby0), (by1, bby1))):
                        for fc in range(nfc):
                            P.emit(PE, lambda e, fc=fc, i=i, half=half, by=by, t2=t2, nfc=nfc: e.matmul(
                                by[:, :], lhsT=aT[:, fc, i * 128:(i + 1) * 128], rhs=t2[:, fc, half * 512:(half + 1) * 512],
                                start=(fc == 0), stop=(fc == nfc - 1)), reads=[B_aT[fc][i // 4], b2], writes=[bby])
                        P.emit(DVE, lambda e, i=i, half=half, by=by: e.scalar_tensor_tensor(
                            out=xs[:, i, half * 512:(half + 1) * 512], in0=by[:, :], scalar=0.5,
                            in1=xs[:, i, half * 512:(half + 1) * 512], op0=ALU.mult, op1=ALU.add),
                            reads=[bby, B_x[i]], writes=[B_x[i]])
                    if post_tile is not None and g == len(groups) - 1:
                        post_tile(i)
                cur = nxt

        def dump(name, src_ap, reads):
            if name in dbg:
                P.final_ops.append(P.emit(POOL, lambda e: e.dma_start(out=dbg[name], in_=src_ap), reads=reads, dma=True))


        xstv = xst_d.rearrange("(i p) d -> p i d", p=128)

        def stash_tile(i):
            P.emit(SP, lambda e, i=i: e.dma_start(out=xstv[:, i, :], in_=xs[:, i, :]), reads=[B_x[i]],
                   writes=[B_stash], dma=True)

        def mixer(s):
            SC_ML = 128.0 ** -0.5
            load_gain("mix_norm")
            norm_to_hT(ar)
            P.barrier()
            arx = Arena(nc, xs_off, xs_off + NT * D * 4)
            m_ar0 = ar.mark()
            mixcat = ar.alloc("mixcat", [128, NT, D], BF16)
            B_mix = [Buf() for _ in range(NT)]
            wring = Ring(ar, "wb", [128, 8, 512], BF16, 2)
            ptring = Ring(ar, "pt", [128, 512], BF16, 4)
            sm = Ring(ar, "sm", [128, 16], F32, 12)
            junk = ar.alloc("mjunk", [128, 128], F32)
            B_junk = Buf()
            m_ar1 = ar.mark()

            LOOK = 2

            deferred = []

            def defer(delay, fn):
                deferred.append([delay, fn])

            def tick():
                for d in deferred:
                    d[0] -= 1
                ready = [d for d in deferred if d[0] <= 0]
                for d in ready:
                    deferred.remove(d)
                for d in ready:
                    d[1]()

            def run_pipeline(steps, fA, fC, look=LOOK, fillers=()):
                n = len(steps)
                fillers = list(fillers)
                for t in range(n + look):
                    if t < n:
                        fA(*steps[t])
                    if t - look >= 0:
                        fC(*steps[t - look])
                    if fillers:
                        fillers.pop(0)()
                    tick()
                while fillers:
                    fillers.pop(0)()
                    tick()
                while deferred:
                    tick()

            def load_w(name, ncols, c0=0):
                t, b = wring.next()
                src = wd[name].rearrange("(k p) m -> p k m", p=128)[:, :, c0:c0 + ncols]
                pool_cast_load(t[:, :, 0:ncols], src, [b])
                return t, b

            def proj_fm(wt, wb, c0, M, evac):
                for tg in range(4):
                    bk, bb = nextbank()
                    for k in range(8):
                        P.emit(PE, lambda e, k=k, tg=tg, bk=bk: e.matmul(
                            bk[0:M, :], lhsT=wt[:, k, c0:c0 + M], rhs=hT[:, k, tg * 512:(tg + 1) * 512],
                            start=(k == 0), stop=(k == 7)), reads=[wb, B_hT[tg]], writes=[bb])
                    evac(tg, bk, bb)

            def proj_tm(wt, wb, c0, N, evac):
                for i in range(NT):
                    bk, bb = nextbank()
                    for k in range(8):
                        P.emit(PE, lambda e, k=k, i=i, bk=bk: e.matmul(
                            bk[:, 0:N], lhsT=hT[:, k, i * 128:(i + 1) * 128], rhs=wt[:, k, c0:c0 + N],
                            start=(k == 0), stop=(k == 7)), reads=[wb, B_hT[i // 4]], writes=[bb])
                    evac(i, bk, bb)

            rowI = ar.alloc("rowI", [4, T], F32)
            rowL = ar.alloc("rowL", [4, T], F32)
            rowA = ar.alloc("rowA", [4, T], F32)
            B_rI, B_rL, B_rA = Buf(), Buf(), Buf()
            ztok = ar.alloc("ztok", [128, NT, 4], F32)
            ecltok = ar.alloc("ecltok", [128, NT, 4], F32)
            B_zt, B_et = Buf(), Buf()
            nmring = Ring(ar, "nmb", [128, 512], F32, 1)
            accring = Ring(ar, "accs", [128, 4, 129], F32, 2)

            wt, wb = load_w("w_gi", 4)
            proj_fm(wt, wb, 0, 4, lambda tg, bk, bb: P.emit(ACT, lambda e: e.activation(
                out=rowI[:, tg * 512:(tg + 1) * 512], in_=bk[0:4, :], func=AF.Identity, bias=b_i[:, 0:1], scale=1.0),
                reads=[bb, B_const], writes=[B_rI]))
            wt, wb = load_w("w_gf", 4)
            proj_fm(wt, wb, 0, 4, lambda tg, bk, bb: P.emit(ACT, lambda e: e.activation(
                out=rowL[:, tg * 512:(tg + 1) * 512], in_=bk[0:4, :], func=AF.Exp, bias=nb_f[:, 0:1], scale=-1.0),
                reads=[bb, B_const], writes=[B_rL]))
            P.emit(ACT, lambda e: e.activation(out=rowL[:], in_=rowL[:], func=AF.Ln, bias=1.0, scale=1.0),
                   reads=[B_rL], writes=[B_rL])
            P.emit(DVE, lambda e: e.tensor_tensor_scan(out=rowA[:], data0=rowL[:], data1=rowL[:], initial=0.0,
                                                       op0=ALU.add, op1=ALU.max), reads=[B_rL], writes=[B_rA])
            P.emit(DVE, lambda e: e.tensor_tensor(out=rowI[:], in0=rowI[:], in1=rowA[:], op=ALU.add),
                   reads=[B_rI, B_rA], writes=[B_rI])
            P.emit(DVE, lambda e: e.tensor_tensor_scan(out=rowL[:], data0=rowI[:], data1=rowI[:], initial=0.0,
                                                       op0=ALU.max, op1=ALU.max), reads=[B_rI, B_rL], writes=[B_rL])
            P.emit(DVE, lambda e: e.tensor_scalar(out=rowL[:], in0=rowL[:], scalar1=-1.0, scalar2=None, op0=ALU.mult),
                   reads=[B_rL], writes=[B_rL])
            P.emit(DVE, lambda e: e.tensor_tensor(out=rowA[:], in0=rowA[:], in1=rowL[:], op=ALU.add),
                   reads=[B_rA, B_rL], writes=[B_rA])
            Vp = arx.alloc("Vp", [128, NT, 4, 130], BF16)
            B_Vp = [Buf() for _ in range(NT)]
            sigmo = arx.alloc("sigmo", [128, NT, 512], BF16)
            B_sig = [Buf() for _ in range(NT)]
            qpre = arx.alloc("qpre", [128, T + 3], F32)
            cacc = arx.alloc("cacc", [128, T], F32)
            qT = arx.alloc("qT", [128, T], BF16)
            kT = arx.alloc("kT", [128, T], BF16)
            B_qpre, B_cacc, B_qT, B_kT = Buf(), Buf(), Buf(), Buf()
            wgring = Ring(arx, "wg", [128, 512], BF16, 4)
            sgring = Ring(arx, "sg", [128, 512], F32, 1)
            P.emit(POOL, lambda e: e.memset(Vp[:], 1.0), writes=B_Vp)
            wt, wb = load_w("w_mv", 512)
            proj_tm(wt, wb, 0, 512, lambda i, bk, bb: P.emit(ACT, lambda e: e.copy(
                out=Vp[:, i, :, 0:128], in_=bk[:, :].rearrange("p (h d) -> p h d", h=4)), reads=[bb], writes=[B_Vp[i]]))
            wt, wb = load_w("w_mo", 512)

            def ev_o(i, bk, bb):
                st, sb_ = sgring.next()
                P.emit(ACT, lambda e: e.activation(out=st[:], in_=bk[:, :], func=AF.Sigmoid), reads=[bb], writes=[sb_])
                P.emit(DVE, lambda e: e.tensor_tensor(out=sigmo[:, i, :], in0=st[:], in1=gnbc[:], op=ALU.mult),
                       reads=[sb_, B_const], writes=[B_sig[i]])
            proj_tm(wt, wb, 0, 512, ev_o)
            P.emit(ACT, lambda e: e.activation(out=rowA[:], in_=rowA[:], func=AF.Exp), reads=[B_rA], writes=[B_rA])
            for (row, B_row, tok, B_tok) in ((rowI, B_rI, ztok, B_zt), (rowA, B_rA, ecltok, B_et)):
                bk, bb = nextbank()
                for i in range(NT):
                    P.emit(PE, lambda e, i=i, bk=bk, row=row: e.transpose(bk[:, i * 4:(i + 1) * 4], row[0:4, i * 128:(i + 1) * 128], ident4[:]),
                           reads=[B_row, B_const], writes=[bb])
                P.emit(DVE, lambda e, bk=bk, tok=tok: e.tensor_copy(out=tok[:].rearrange("p i h -> p (i h)"), in_=bk[:, 0:64]),
                       reads=[bb], writes=[B_tok])


            for h in range(4):
                for (wname, dstT, B_dst, cbase) in (("w_mq", qT, B_qT, 0), ("w_mk", kT, B_kT, 4)):
                    wt, wb = load_w(wname, 128, c0=h * 128)
                    P.emit(DVE, lambda e: e.memset(qpre[:, 0:3], 0.0), writes=[B_qpre])
                    proj_fm(wt, wb, 0, 128, lambda tg, bk, bb: P.emit(ACT, lambda e: e.copy(
                        out=qpre[:, 3 + tg * 512:3 + (tg + 1) * 512], in_=bk[:, :]), reads=[bb], writes=[B_qpre]))
                    cw = convw[:, cbase + h, :]
                    P.emit(DVE, lambda e, cw=cw: e.tensor_scalar(out=cacc[:], in0=qpre[:, 3:3 + T], scalar1=cw[:, 3:4],
                                                                 scalar2=None, op0=ALU.mult),
                           reads=[B_qpre, B_const], writes=[B_cacc])
                    for j in (2, 1, 0):
                        P.emit(DVE, lambda e, cw=cw, j=j: e.scalar_tensor_tensor(
                            out=cacc[:], in0=qpre[:, j:j + T], scalar=cw[:, j:j + 1], in1=cacc[:],
                            op0=ALU.mult, op1=ALU.add), reads=[B_qpre, B_cacc, B_const], writes=[B_cacc])
                    P.emit(ACT, lambda e, dstT=dstT: e.activation(out=dstT[:], in_=cacc[:], func=AF.Silu),
                           reads=[B_cacc], writes=[B_dst])
                steps = []
                for I in range(4):
                    st_I = {}
                    for j in range(4 * I + 4):
                        steps.append((I, j, st_I))

                def ml_A(I, j, st, h=h):
                    if j == 0:
                        bk, bb = nextbank()
                        P.emit(PE, lambda e, bk=bk: e.matmul(bk[:, :], lhsT=sel4[0:4, h * 128:(h + 1) * 128],
                                                             rhs=rowL[0:4, I * 512:(I + 1) * 512], start=True, stop=True),
                               reads=[B_const, B_rL], writes=[bb])
                        nmb, nmbb = nmring.next()
                        P.emit(ACT, lambda e, bk=bk, nmb=nmb: e.copy(out=nmb[:], in_=bk[:, :]), reads=[bb], writes=[nmbb])
                        st["nmb"] = (nmb, nmbb)
                    nmb, nmbb = st["nmb"]
                    qd = j - 4 * I
                    c0 = max(0, qd) * 128
                    bs, bbs = nextbank()
                    P.emit(PE, lambda e, bs=bs: e.matmul(bs[:, c0:512], lhsT=kT[:, j * 128:(j + 1) * 128],
                                                         rhs=qT[:, I * 512 + c0:(I + 1) * 512], start=True, stop=True),
                           reads=[B_kT, B_qT], writes=[bbs])
                    wg, wgb = wgring.next()
                    P.emit(ACT, lambda e, wg=wg, nmb=nmb: e.activation(
                        out=wg[:, c0:512], in_=nmb[:, c0:512], func=AF.Exp, bias=ztok[:, j, h:h + 1], scale=1.0),
                        reads=[nmbb, B_zt], writes=[wgb])
                    pt, pb = ptring.next()
                    P.emit(DVE, lambda e, pt=pt, bs=bs, wg=wg: e.scalar_tensor_tensor(
                        out=pt[:, c0:512], in0=bs[:, c0:512], scalar=SC_ML, in1=wg[:, c0:512], op0=ALU.mult, op1=ALU.mult),
                        reads=[bbs, wgb], writes=[pb])
                    if qd >= 0:
                        P.emit(DVE, lambda e, pt=pt: e.tensor_tensor(
                            out=pt[:, qd * 128:(qd + 1) * 128], in0=pt[:, qd * 128:(qd + 1) * 128], in1=trile[:],
                            op=ALU.mult), reads=[pb, B_const], writes=[pb])
                    st[("pt", j)] = (pt, pb)

                def ml_C(I, j, st, h=h):
                    pt, pb = st.pop(("pt", j))
                    if j == 0:
                        st["accb"] = [nextbank(hold=True), nextbank(hold=True)]
                        st["fresh"] = [True, True]
                    accb = st["accb"]
                    qd = j - 4 * I
                    for q in range(max(0, qd), 4):
                        ab, abb = accb[q // 2]
                        o0 = (q % 2) * 129
                        first = st["fresh"][q // 2]
                        st["fresh"][q // 2] = False
                        P.emit(PE, lambda e, ab=ab, o0=o0, pt=pt, q=q, first=first: e.matmul(
                            ab[:, o0:o0 + 129], lhsT=pt[:, q * 128:(q + 1) * 128], rhs=Vp[:, j, h, 0:129],
                            start=first, stop=(j == 4 * I + q), skip_group_check=True), reads=[pb, B_Vp[j]], writes=[abb])
                    if j != 4 * I + 3:
                        return
                    acs, acsb = accring.next()
                    for half in range(2):
                        ab, abb = accb[half]
                        P.emit(DVE, lambda e, ab=ab, half=half, acs=acs: e.tensor_copy(
                            out=acs[:, 2 * half:2 * half + 2, :], in_=ab[:, 0:258].rearrange("p (q c) -> p q c", c=129)),
                            reads=[abb], writes=[acsb])
                        release(abb)
                    d1, d1b = sm.next()
                    d2, d2b = sm.next()
                    d3, d3b = sm.next()
                    i0 = 4 * I

                    def s1():
                        P.emit(DVE, lambda e: e.tensor_scalar(out=d1[:, 0:4], in0=acs[:, :, 128], scalar1=-1.0, scalar2=None, op0=ALU.mult),
                               reads=[acsb], writes=[d1b])
                        P.emit(DVE, lambda e: e.tensor_tensor(out=d1[:, 0:4], in0=d1[:, 0:4], in1=acs[:, :, 128], op=ALU.max),
                               reads=[acsb, d1b], writes=[d1b])
                        P.emit(DVE, lambda e: e.tensor_tensor(out=d1[:, 0:4], in0=d1[:, 0:4], in1=ecltok[:, i0:i0 + 4, h], op=ALU.max),
                               reads=[d1b, B_et], writes=[d1b])
                        P.emit(DVE, lambda e: e.reciprocal(out=d1[:, 0:4], in_=d1[:, 0:4]), reads=[d1b], writes=[d1b])
                        P.emit(DVE, lambda e: e.memset(d2[:, 0:4], 0.0), writes=[d2b])

                    def s2():
                        for q in range(4):
                            P.emit(ACT, lambda e, q=q: e.activation(
                                out=junk[:], in_=acs[:, q, 0:128], func=AF.Square, scale=d1[:, q:q + 1], accum_out=d2[:, q:q + 1]),
                                reads=[acsb, d1b], writes=[B_junk, d2b])

                    def s3():
                        P.emit(ACT, lambda e: e.activation(out=d3[:, 0:4], in_=d2[:, 0:4], func=AF.Ln, bias=epsb[:, 0:1], scale=1.0 / 128),
                               reads=[d2b, B_const], writes=[d3b])
                        P.emit(ACT, lambda e: e.activation(out=d3[:, 0:4], in_=d3[:, 0:4], func=AF.Exp, scale=-0.5),
                               reads=[d3b], writes=[d3b])

                    def s4():
                        P.emit(DVE, lambda e: e.tensor_tensor(out=d3[:, 0:4], in0=d3[:, 0:4], in1=d1[:, 0:4], op=ALU.mult),
                               reads=[d3b, d1b], writes=[d3b])
                        for q in range(4):
                            i = i0 + q
                            P.emit(DVE, lambda e, q=q, i=i: e.scalar_tensor_tensor(
                                out=mixcat[:, i, h * 128:(h + 1) * 128], in0=acs[:, q, 0:128], scalar=d3[:, q:q + 1],
                                in1=sigmo[:, i, h * 128:(h + 1) * 128], op0=ALU.mult, op1=ALU.mult),
                                reads=[acsb, d3b, B_sig[i]], writes=[B_mix[i]])
                    defer(1, s1)
                    defer(2, s2)
                    defer(3, s3)
                    defer(4, s4)

                run_pipeline(steps, ml_A, ml_C, look=3)
            dump("d_ztok", ztok[:].rearrange("p i h -> p (i h)"), [B_zt])
            dump("d_ecl", ecltok[:].rearrange("p i h -> p (i h)"), [B_et])
            dump("d_qT", qT[:], [B_qT])
            dump("d_kT", kT[:], [B_kT])
            dump("d_nm", rowL[:], [B_rL])
            P.barrier()
            ar.reset(m_ar1)
            arx.reset(xs_off)

            SC_N = 0.125
            VV = arx.alloc("VV", [128, NT, 4, 66], BF16)
            m_arx1 = arx.mark()
            B_VV = [Buf() for _ in range(NT)]
            gates = ar.alloc("gates", [128, NT, 24], F32)
            B_gates = Buf()
            kcmpT = [ar.alloc(f"kcmpT{g}", [128, 128], BF16) for g in range(2)]
            Rg = [ar.alloc(f"Rg{g}", [128, 97], BF16) for g in range(2)]
            B_kcmp = [Buf(), Buf()]
            B_Rg = [Buf(), Buf()]
            B_hn = [Buf() for _ in range(NT)]
            imp = ar.alloc("imp", [128, NT, 32], F32)
            B_imp = [Buf() for _ in range(NT)]
            selT = ar.alloc("selT", [32, T], BF16)
            B_nst = [Buf() for _ in range(4)]
            scr = Ring(ar, "scr", [128, 32], F32, 4)
            nsr = Ring(ar, "nsr", [128, 32], BF16, 8)
            hidT = [ar.alloc(f"hidT{i}", [128, 2, 128], BF16) for i in range(2)]
            B_hid = [Buf(), Buf()]
            biasT = ar.alloc("biasT", [128, 2], F32)
            B_bias = Buf()
            b1t = ar.alloc("b1t", [128, 2], F32)
            B_b1t = Buf()
            w2d = ar.alloc("w2d", [128, 2, 128], BF16)
            w2v = ar.alloc("w2v", [128, 2, 64], BF16)
            peT = ar.alloc("peT", [128, 16], BF16)
            B_w2 = Buf()
            B_pe = Buf()

            P.emit(POOL, lambda e: e.memset(VV[:], 1.0), writes=B_VV)
            wt, wb = load_w("w_vtok", 280)

            def ev_vt(i, bk, bb):
                P.emit(ACT, lambda e: e.copy(out=VV[:, i, :, 0:64], in_=bk[:, 0:256].rearrange("p (a d) -> p a d", a=4)),
                       reads=[bb], writes=[B_VV[i]])
                P.emit(ACT, lambda e: e.activation(out=gates[:, i, :], in_=bk[:, 256:280], func=AF.Sigmoid),
                       reads=[bb], writes=[B_gates])
            proj_tm(wt, wb, 0, 280, ev_vt)

            kvcT = arx.alloc("kvcT", [128, 4, T], BF16)
            B_kvc = [Buf() for _ in range(4)]
            w1ring = Ring(arx, "w1r", [128, 16, 256], BF16, 2)
            wt, wb = load_w("w_kvc2", 512)

            def ev_kvc(tg, bk, bb, idx):
                P.emit(ACT, lambda e: e.copy(out=kvcT[0:64, idx, tg * 512:(tg + 1) * 512], in_=bk[0:64, :]),
                       reads=[bb], writes=[B_kvc[idx]])
                if tg == 0:
                    P.emit(ACT, lambda e: e.copy(out=kvcT[64:128, idx, 0:511], in_=bk[64:128, 1:512]),
                           reads=[bb], writes=[B_kvc[idx]])
                else:
                    P.emit(ACT, lambda e: e.copy(out=kvcT[64:128, idx, tg * 512 - 1:(tg + 1) * 512 - 1], in_=bk[64:128, :]),
                           reads=[bb], writes=[B_kvc[idx]])
            for idx in range(4):
                proj_fm(wt, wb, idx * 128, 128, lambda tg, bk, bb, idx=idx: ev_kvc(tg, bk, bb, idx))
            pool_cast_load(w2d[:], wd["cmp_k_w2d"].rearrange("(c p) m -> p c m", p=128), [B_w2])
            pool_cast_load(w2v[:], wd["cmp_v_w2"].rearrange("(c p) m -> p c m", p=128), [B_w2])
            for g in range(2):
                P.emit(DVE, lambda e, g=g: e.tensor_copy(out=Rg[g][:, 0:33], in_=ov1[:]), reads=[B_const], writes=[B_Rg[g]])
            for kvi, kv in enumerate(("k", "v")):
                w1t, w1b = w1ring.next()
                pool_cast_load(w1t[:], wd[f"cmp_{kv}_w1"].rearrange("(c p) h -> p c h", p=128), [w1b])
                pool_cast_load(peT[:], wd[f"cmp_{kv}_peT2"][:, :], [B_pe])
                sp_load(b1t[:], wd[f"cmp_{kv}_b1"][:, :], [B_b1t])
                bk, bb = nextbank()
                for hc in range(2):
                    for c in range(16):
                        P.emit(PE, lambda e, bk=bk, hc=hc, c=c, w1t=w1t: e.matmul(
                            bk[:, hc:hc + 1], lhsT=w1t[:, c, hc * 128:(hc + 1) * 128], rhs=peT[:, c:c + 1],
                            start=(c == 0), stop=(c == 15)), reads=[w1b, B_pe], writes=[bb])
                P.emit(DVE, lambda e, bk=bk: e.tensor_tensor(out=biasT[:], in0=bk[:, 0:2], in1=b1t[:], op=ALU.add),
                       reads=[bb, B_b1t], writes=[B_bias])
                for g in range(2):
                    idx = kvi * 2 + g
                    ht_, hb_ = hidT[g], B_hid[g]
                    for hc in range(2):
                        bk, bb = nextbank()
                        for c in range(16):
                            P.emit(PE, lambda e, bk=bk, hc=hc, c=c, idx=idx, w1t=w1t: e.matmul(
                                bk[:, 0:127], lhsT=w1t[:, c, hc * 128:(hc + 1) * 128],
                                rhs=kvcT[:, idx, 2 * c:2 * c + 16 * 126 + 1:16], start=(c == 0), stop=(c == 15)),
                                reads=[w1b, B_kvc[idx]], writes=[bb])
                        P.emit(ACT, lambda e, bk=bk, hc=hc, ht_=ht_: e.activation(
                            out=ht_[:, hc, 0:127], in_=bk[:, 0:127], func=AF.Silu, bias=biasT[:, hc:hc + 1], scale=1.0),
                            reads=[bb, B_bias], writes=[hb_])
                    bk, bb = nextbank()
                    if kv == "k":
                        for hc in range(2):
                            P.emit(PE, lambda e, bk=bk, hc=hc, ht_=ht_: e.matmul(
                                bk[:, 0:127], lhsT=w2d[:, hc, :], rhs=ht_[:, hc, 0:127], start=(hc == 0), stop=(hc == 1)),
                                reads=[B_w2, hb_], writes=[bb])
                        P.emit(ACT, lambda e, bk=bk, g=g: e.copy(out=kcmpT[g][:, 0:127], in_=bk[:, 0:127]),
                               reads=[bb], writes=[B_kcmp[g]])
                    else:
                        for hc in range(2):
                            P.emit(PE, lambda e, bk=bk, hc=hc, ht_=ht_: e.matmul(
                                bk[0:127, 0:64], lhsT=ht_[:, hc, 0:127], rhs=w2v[:, hc, :], start=(hc == 0), stop=(hc == 1)),
                                reads=[B_w2, hb_], writes=[bb])
                        P.emit(ACT, lambda e, bk=bk, g=g: e.copy(out=Rg[g][0:127, 33:97], in_=bk[0:127, 0:64]),
                               reads=[bb], writes=[B_Rg[g]])
            P.barrier()
            arx.reset(m_arx1)

            hnacc = arx.alloc("hnacc", [128, NT, 256], F32)
            nqT = arx.alloc("nqT", [128, 2, T], BF16)
            ksdT = arx.alloc("ksdT", [128, T], BF16)
            kwdT = arx.alloc("kwdT", [128, T], BF16)
            B_nq, B_ksd, B_kwd = Buf(), Buf(), Buf()
            maskS = [arx.alloc(f"maskS{j}", [128, 512], BF16) for j in range(16)]
            B_mask = [Buf() for _ in range(16)]
            ptring2 = Ring(ar, "pt2", [128, 512], BF16, 8)
            accs2 = Ring(ar, "accs2", [128, 260], F32, 4)

            def evac_branch(ab, abb, I, H, hh, br, first):
                av = ab[:, 0:260].rearrange("p (q c) -> p q c", c=65)
                d1, d1b = sm.next()
                P.emit(DVE, lambda e: e.reciprocal(out=d1[:, 0:4], in_=av[:, :, 64]), reads=[abb], writes=[d1b])
                P.emit(DVE, lambda e: e.tensor_tensor(out=d1[:, 0:4], in0=d1[:, 0:4], in1=gates[:, 4 * I:4 * I + 4, H * 3 + br], op=ALU.mult),
                       reads=[d1b, B_gates], writes=[d1b])
                for q in range(4):
                    i = 4 * I + q
                    dst = hnacc[:, i, hh * 64:(hh + 1) * 64]
                    if first:
                        P.emit(DVE, lambda e, q=q, dst=dst: e.tensor_scalar(out=dst, in0=av[:, q, 0:64], scalar1=d1[:, q:q + 1],
                                                                        scalar2=None, op0=ALU.mult),
                               reads=[abb, d1b], writes=[B_hn[i]])
                    else:
                        P.emit(DVE, lambda e, q=q, dst=dst: e.scalar_tensor_tensor(out=dst, in0=av[:, q, 0:64], scalar=d1[:, q:q + 1],
                                                                               in1=dst, op0=ALU.mult, op1=ALU.add),
                               reads=[abb, d1b, B_hn[i]], writes=[B_hn[i]])

            for g in range(2):
                for c in range(2):
                    wt, wb = load_w("w_nq", 128, c0=g * 256 + c * 128)
                    proj_fm(wt, wb, 0, 128, lambda tg, bk, bb, c=c: P.emit(ACT, lambda e: e.copy(
                        out=nqT[:, c, tg * 512:(tg + 1) * 512], in_=bk[:, :]), reads=[bb], writes=[B_nq]))
                wt, wb = load_w("w_ksd", 128, c0=g * 128)
                proj_fm(wt, wb, 0, 128, lambda tg, bk, bb: P.emit(ACT, lambda e: e.copy(
                    out=ksdT[:, tg * 512:(tg + 1) * 512], in_=bk[:, :]), reads=[bb], writes=[B_ksd]))
                wt, wb = load_w("w_kwd", 128, c0=g * 128)
                proj_fm(wt, wb, 0, 128, lambda tg, bk, bb: P.emit(ACT, lambda e: e.copy(
                    out=kwdT[:, tg * 512:(tg + 1) * 512], in_=bk[:, :]), reads=[bb], writes=[B_kwd]))
                if g == 1:
                    for i in range(8):
                        stg = hT[:, i, :].bitcast(F32)
                        P.emit(SP, lambda e, i=i, stg=stg: e.dma_start(out=stg, in_=xstv[:, i, :]), reads=[B_stash],
                               writes=B_hT, dma=True)
                P.emit(POOL, lambda e: e.memset(imp[:], 0.0), writes=B_imp)
                csteps = [(hh, I, {}) for hh in range(4) for I in range(4)]

                def c_A(hh, I, st, g=g):
                    c, half = hh // 2, hh % 2
                    r0 = half * 64
                    bs, bbs = nextbank()
                    P.emit(PE, lambda e: e.matmul(
                        bs[0:127, :], lhsT=kcmpT[g][r0:r0 + 64, 0:127], rhs=nqT[r0:r0 + 64, c, I * 512:(I + 1) * 512],
                        start=True, stop=True), reads=[B_kcmp[g], B_nq], writes=[bbs])
                    pt, pb = ptring.next()
                    P.emit(ACT, lambda e: e.activation(out=pt[0:127, :], in_=bs[0:127, :], func=AF.Exp, scale=SC_N),
                           reads=[bbs], writes=[pb])
                    P.emit(DVE, lambda e: e.tensor_tensor(out=pt[0:127, :], in0=pt[0:127, :],
                                                          in1=cmpmask[0:127, I * 512:(I + 1) * 512], op=ALU.mult),
                           reads=[pb, B_const], writes=[pb])
                    st["pt"] = (pt, pb)

                def c_C(hh, I, st, g=g):
                    H = 4 * g + hh
                    pt, pb = st["pt"]
                    ab, abb = nextbank()
                    for q in range(4):
                        P.emit(PE, lambda e, q=q: e.matmul(
                            ab[:, q * 97:(q + 1) * 97], lhsT=pt[0:127, q * 128:(q + 1) * 128], rhs=Rg[g][0:127, 0:97],
                            start=True, stop=True), reads=[pb, B_Rg[g]], writes=[abb])
                    av = ab[:, 0:388].rearrange("p (q c) -> p q c", c=97)
                    d1, d1b = sm.next()
                    d2, d2b = sm.next()

                    def ev():
                        P.emit(DVE, lambda e: e.tensor_scalar(out=d1[:, 0:4], in0=av[:, :, 32], scalar1=1e-30, scalar2=None, op0=ALU.max),
                               reads=[abb], writes=[d1b])
                        P.emit(DVE, lambda e: e.reciprocal(out=d1[:, 0:4], in_=d1[:, 0:4]), reads=[d1b], writes=[d1b])
                        P.emit(DVE, lambda e: e.tensor_tensor(
                            out=d2[:, 0:4], in0=d1[:, 0:4], in1=gates[:, 4 * I:4 * I + 4, H * 3 + 0], op=ALU.mult),
                            reads=[d1b, B_gates], writes=[d2b])
                        tmpv = junk[:, 0:128].rearrange("p (q c) -> p q c", c=32)
                        P.emit(DVE, lambda e: e.tensor_tensor(
                            out=tmpv, in0=av[:, :, 0:32], in1=d1[:, 0:4].unsqueeze(2).to_broadcast([128, 4, 32]), op=ALU.mult),
                            reads=[abb, d1b], writes=[B_junk])
                        P.emit(DVE, lambda e: e.tensor_tensor(
                            out=imp[:, 4 * I:4 * I + 4, :], in0=imp[:, 4 * I:4 * I + 4, :], in1=tmpv, op=ALU.add),
                            reads=[B_junk] + B_imp[4 * I:4 * I + 4], writes=B_imp[4 * I:4 * I + 4])
                        P.emit(DVE, lambda e: e.tensor_tensor(
                            out=hnacc[:, 4 * I:4 * I + 4, hh * 64:(hh + 1) * 64], in0=av[:, :, 33:97],
                            in1=d2[:, 0:4].unsqueeze(2).to_broadcast([128, 4, 64]), op=ALU.mult),
                            reads=[abb, d2b], writes=B_hn[4 * I:4 * I + 4])
                    defer(1, ev)

                run_pipeline(csteps, c_A, c_C)
                sel_fillers = []
                for I in range(4):
                    hold_I = {}
                    for q in range(4):
                        def f_sel(I=I, q=q, hold_I=hold_I):
                            i = 4 * I + q
                            sc, scb = scr.next()
                            m8, m8b = sm.next()
                            wk, wkb = scr.next()
                            ns, nsb = nsr.next()
                            P.emit(DVE, lambda e: e.tensor_tensor(out=sc[:], in0=imp[:, i, :], in1=selvis[:, i, :], op=ALU.mult),
                                   reads=[B_imp[i], B_const], writes=[scb])
                            P.emit(DVE, lambda e: e.tensor_tensor(out=sc[:], in0=sc[:], in1=seladd[:, i, :], op=ALU.add),
                                   reads=[scb, B_const], writes=[scb])
                            P.emit(DVE, lambda e: e.max(out=m8[:, 0:8], in_=sc[:]), reads=[scb], writes=[m8b])
                            P.emit(DVE, lambda e: e.match_replace(out=wk[:], in_to_replace=m8[:, 0:8], in_values=sc[:],
                                                                  imm_value=-3.0e38), reads=[scb, m8b], writes=[wkb])
                            P.emit(DVE, lambda e: e.max(out=m8[:, 8:16], in_=wk[:]), reads=[wkb, m8b], writes=[m8b])
                            P.emit(DVE, lambda e: e.tensor_scalar(out=ns[:], in0=sc[:], scalar1=m8[:, 15:16], scalar2=None,
                                                                  op0=ALU.is_ge), reads=[scb, m8b], writes=[nsb])
                            hold_I[q] = (ns, nsb)
                            if q == 3:
                                def f_tr():
                                    bk, bb = nextbank(hold=True)
                                    bkb = bk[:, :].bitcast(BF16)
                                    for qq in range(4):
                                        ns_, nsb_ = hold_I[qq]
                                        P.emit(PE, lambda e, qq=qq, ns_=ns_: e.transpose(bkb[0:32, qq * 128:(qq + 1) * 128], ns_[:], ident_b[:]),
                                               reads=[nsb_, B_const], writes=[bb])

                                    def f_cp():
                                        P.emit(ACT, lambda e: e.copy(out=selT[0:32, I * 512:(I + 1) * 512], in_=bkb[0:32, 0:512]),
                                               reads=[bb], writes=[B_nst[I]])
                                        release(bb)
                                    defer(2, f_cp)
                                defer(2, f_tr)
                        sel_fillers.append(f_sel)
                steps = []
                for br in (2, 1):
                    if br not in DEBUG_BRANCHES:
                        continue
                    if br == 1:
                        steps.append(None)
                    for I in range(4):
                        for p in range(2):
                            st_I = {}
                            jlo = 0 if br == 1 else max(0, 4 * I - 4)
                            for j in range(jlo, 4 * I + 4):
                                steps.append((br, I, p, j, jlo, st_I))

                def n_A(br, I, p, j, jlo, st, g=g):
                    kTt, B_k = (ksdT, B_ksd) if br == 1 else (kwdT, B_kwd)
                    use_mask = (br == 1 and I >= 2)
                    qlo = max(0, j - 4 * I)
                    qhi = 3 if br == 1 else min(3, j - 4 * I + 4)
                    c0, c1 = qlo * 128, (qhi + 1) * 128
                    if use_mask and p == 0:
                        bk, bb = nextbank()
                        P.emit(PE, lambda e, bk=bk: e.matmul(
                            bk[:, c0:c1], lhsT=exm[0:32, j * 128:(j + 1) * 128], rhs=selT[0:32, I * 512 + c0:I * 512 + c1],
                            start=True, stop=True), reads=[B_const, B_nst[I]], writes=[bb])
                        P.emit(DVE, lambda e, bk=bk: e.tensor_copy(out=maskS[j][:, c0:c1], in_=bk[:, c0:c1]),
                               reads=[bb], writes=[B_mask[j]])
                        if j - 4 * I >= 0:
                            P.emit(DVE, lambda e: e.tensor_tensor(
                                out=maskS[j][:, c0:c0 + 128], in0=maskS[j][:, c0:c0 + 128], in1=trile[:], op=ALU.mult),
                                reads=[B_mask[j], B_const], writes=[B_mask[j]])
                    bsl = []
                    for half in range(2):
                        r0 = half * 64
                        bs, bbs = nextbank()
                        P.emit(PE, lambda e, bs=bs, r0=r0: e.matmul(
                            bs[:, c0:c1], lhsT=kTt[r0:r0 + 64, j * 128:(j + 1) * 128], rhs=nqT[r0:r0 + 64, p, I * 512 + c0:I * 512 + c1],
                            start=True, stop=True), reads=[B_k, B_nq], writes=[bbs])
                        bsl.append((bs, bbs))
                    ptl = []
                    for half in range(2):
                        bs, bbs = bsl[half]
                        pt, pb = ptring2.next()
                        P.emit(ACT, lambda e, pt=pt, bs=bs: e.activation(out=pt[:, c0:c1], in_=bs[:, c0:c1], func=AF.Exp, scale=SC_N),
                               reads=[bbs], writes=[pb])
                        if use_mask:
                            P.emit(DVE, lambda e, pt=pt: e.tensor_tensor(
                                out=pt[:, c0:c1], in0=pt[:, c0:c1], in1=maskS[j][:, c0:c1], op=ALU.mult),
                                reads=[pb, B_mask[j]], writes=[pb])
                        else:
                            for q in range(qlo, qhi + 1):
                                r = j - (4 * I + q)
                                msk = trile if r == 0 else (trigt if (r == -4 and br == 2) else None)
                                if msk is not None:
                                    P.emit(DVE, lambda e, pt=pt, q=q, msk=msk: e.tensor_tensor(
                                        out=pt[:, q * 128:(q + 1) * 128], in0=pt[:, q * 128:(q + 1) * 128], in1=msk[:], op=ALU.mult),
                                        reads=[pb, B_const], writes=[pb])
                        ptl.append((pt, pb))
                    st[("pt", j)] = (ptl, qlo, qhi)

                def n_C(br, I, p, j, jlo, st, g=g):
                    vidx = 2 * g + (0 if br == 1 else 1)
                    ptl, qlo, qhi = st.pop(("pt", j))
                    if j == jlo:
                        st["ab"] = [nextbank(hold=True), nextbank(hold=True)]
                        st["fresh"] = [True, True]
                    for half in range(2):
                        pt, pb = ptl[half]
                        ab, abb = st["ab"][half]
                        for q in range(qlo, qhi + 1):
                            i = 4 * I + q
                            first = st["fresh"][half]
                            st["fresh"][half] = False
                            P.emit(PE, lambda e, ab=ab, q=q, pt=pt, i=i, first=first: e.matmul(
                                ab[:, q * 65:(q + 1) * 65], lhsT=pt[:, q * 128:(q + 1) * 128], rhs=VV[:, j, vidx, 0:65],
                                start=first, stop=(j == i), skip_group_check=True), reads=[pb, B_VV[j]], writes=[abb])
                    if j == 4 * I + 3:
                        for half in range(2):
                            hh = 2 * p + half
                            ab, abb = st["ab"][half]
                            cs, csb = accs2.next()
                            P.emit(DVE, lambda e, ab=ab, cs=cs: e.tensor_copy(out=cs[:, 0:260], in_=ab[:, 0:260]),
                                   reads=[abb], writes=[csb])
                            release(abb)
                            defer(1, lambda cs=cs, csb=csb, hh=hh: evac_branch(cs, csb, I, 4 * g + hh, hh, br, False))

                if None in steps:
                    k = steps.index(None)
                    run_pipeline(steps[:k], n_A, n_C, fillers=sel_fillers)
                    run_pipeline(steps[k + 1:], n_A, n_C)
                else:
                    run_pipeline(steps, n_A, n_C, fillers=sel_fillers)
                for i in range(NT):
                    P.emit(ACT, lambda e, i=i, g=g: e.copy(out=mixcat[:, i, 512 + g * 256:512 + (g + 1) * 256], in_=hnacc[:, i, :]),
                           reads=[B_hn[i]], writes=[B_mix[i]])
            dump("d_kcmp0", kcmpT[0][:], [B_kcmp[0]])
            dump("d_Rg0", Rg[0][:], [B_Rg[0]])
            dump("d_imp", imp[:].rearrange("p i b -> p (i b)"), B_imp)
            dump("d_nst", selT[:], B_nst)
            dump("d_gates", gates[:].rearrange("p i b -> p (i b)"), [B_gates])
            dump("d_mix", mixcat[:].rearrange("p i d -> p (i d)"), B_mix)
            P.barrier()
            ar.reset(m_ar1)

            for i in range(8, NT):
                if i % 2 == 0:
                    sp_load(xs[:, i, :], xstv[:, i, :], [B_x[i]], reads=[B_stash])
                else:
                    P.emit(POOL, lambda e, i=i: e.dma_start(out=xs[:, i, :], in_=xstv[:, i, :]), writes=[B_x[i]],
                           reads=[B_stash], dma=True)
            wout = ar.alloc("wout", [128, 8, D], BF16)
            B_wout = Buf()
            wov = wd["w_out"].rearrange("(k p) m -> p k m", p=128)
            pool_cast_load(wout[:, :, 0:512], wov[:, :, 0:512], [B_wout])
            pool_cast_load(wout[:, :, 512:1024], wov[:, :, 512:1024], [B_wout])
            mcr = Ring(ar, "mcT", [128, 8, 128], BF16, 2)
            for i in range(NT):
                bk, bb = nextbank()
                bkb = bk[:, :].bitcast(BF16)
                for k in range(8):
                    P.emit(PE, lambda e, k=k, i=i, bkb=bkb: e.transpose(bkb[:, k * 128:(k + 1) * 128], mixcat[:, i, k * 128:(k + 1) * 128], ident_b[:]),
                           reads=[B_mix[i], B_const], writes=[bb])
                mc, mcb = mcr.next()
                P.emit(ACT, lambda e, mc=mc, bkb=bkb: e.copy(out=mc[:].rearrange("p k t -> p (k t)"), in_=bkb), reads=[bb], writes=[mcb])
                for half in range(2):
                    by, bby = nextbank()
                    for k in range(8):
                        P.emit(PE, lambda e, k=k, half=half, by=by, mc=mc: e.matmul(
                            by[:, :], lhsT=mc[:, k, :], rhs=wout[:, k, half * 512:(half + 1) * 512], start=(k == 0), stop=(k == 7)),
                            reads=[mcb, B_wout], writes=[bby])
                    xin = hT[:, i, :].bitcast(F32)[:, half * 512:(half + 1) * 512] if i < 8 else xs[:, i, half * 512:(half + 1) * 512]
                    P.emit(DVE, lambda e, i=i, half=half, by=by, xin=xin: e.tensor_tensor(
                        out=xs[:, i, half * 512:(half + 1) * 512], in0=by[:, :], in1=xin, op=ALU.add),
                        reads=[bby, B_x[i]] + (B_hT if i < 8 else []), writes=[B_x[i]])
            load_gain("ffn2_norm")
            norm_to_hT(ar)
            ar.reset(m_ar0)
            dump("d_x2", xs[:].rearrange("p i d -> p (i d)"), B_x)
            P.barrier()

        for s in range(nseq):
            xv = x_d[s].rearrange("(i p) d -> p i d", p=128)
            for i in range(NT):
                sp_load(xs[:, i, :], xv[:, i, :], [B_x[i]])
            load_gain("ffn1_norm")
            norm_begin()
            ffn(wd["ffn1_w1"], wd["ffn1_w3"], wd["ffn1_w2"], ar, pre_tg=lambda tg: norm_group(4 * tg),
                post_tile=(stash_tile if do_mixer else None))
            if s == 0:
                dump("d_x1", xs[:].rearrange("p i d -> p (i d)"), B_x)
            P.barrier()
            if do_mixer:
                ar.reset(phase_mark)
                mixer(s)
            if not do_mixer:
                load_gain("ffn2_norm")
                norm_to_hT(ar)
            ffn(wd["ffn2_w1"], wd["ffn2_w3"], wd["ffn2_w2"], ar)
            load_gain("final_norm")
            junk = norm_junk
            ot = ffn_bufs["ot"]
            P.emit(DVE, lambda e: e.memset(ss[:], 0.0), writes=[B_ss])
            ov = out_d[s].rearrange("(i p) d -> p i d", p=128)
            for i in range(NT):
                jt, jb = junk.next()
                P.emit(ACT, lambda e, i=i, jt=jt: e.activation(out=jt[:], in_=xs[:, i, :], func=AF.Square,
                                                               accum_out=ss[:, i:i + 1]),
                       reads=[B_x[i]], writes=[jb, B_ss])
            rms_rstd_grp(1.0 / D, 0, NT)
            for i in range(NT):
                o_t, o_b = ot.next()
                P.emit(DVE, lambda e, i=i, o_t=o_t: e.scalar_tensor_tensor(out=o_t[:], in0=xs[:, i, :], scalar=rstd[:, i:i + 1],
                                                                           in1=gbc[:], op0=ALU.mult, op1=ALU.mult),
                       reads=[B_x[i], B_rstd, B_gbc], writes=[o_b])
                P.final_ops.append(P.emit(SP, lambda e, i=i, o_t=o_t, ov=ov: e.dma_start(out=ov[:, i, :], in_=o_t[:]),
                                          reads=[o_b], dma=True))

        sems = {}
        for nm in P.sem_names():
            sems[nm] = es.enter_context(nc.semaphore("_".join(str(v) for v in nm)))
        P.build(sems)
        with nc.Block() as block:
            @block.tensor
            def _(e):
                P.replay(PE, e)

            @block.scalar
            def _(e):
                P.replay(ACT, e)

            @block.vector
            def _(e):
                P.replay(DVE, e)

            @block.gpsimd
            def _(e):
                P.replay(POOL, e)

            @block.sync
            def _(e):
                P.replay(SP, e)
    return nc


def _prep_weights(inp):
    f = lambda a: np.ascontiguousarray(np.asarray(a, dtype=np.float32))
    w = {}
    for ff in ("ffn1", "ffn2"):
        for nm in ("w1", "w3", "w2"):
            w[f"{ff}_{nm}"] = f(inp[f"{ff}_{nm}"][0])
        w[f"{ff}_norm"] = f(inp[f"{ff}_norm"][0][None, :])
    w["mix_norm"] = f(inp["mix_norm"][0][None, :])
    w["final_norm"] = f(np.asarray(inp["final_norm"])[None, :])
    wi = np.asarray(inp["w_in"][0], dtype=np.float32)
    sizes = [512] * 4 + [4] * 2 + [512] + [128] * 6 + [24]
    offs = np.concatenate([[0], np.cumsum(sizes)])
    parts = [wi[:, offs[i]:offs[i + 1]] for i in range(len(sizes))]
    mq, mk, mv, mo, mi, mf, nq, kc, vc, ks, vs, kw, vw, ng = parts
    w["w_gi"] = f(mi)
    w["w_gf"] = f(mf)
    w["w_mq"] = f(mq)
    w["w_mk"] = f(mk)
    w["w_mv"] = f(mv)
    w["w_mo"] = f(mo)
    w["w_nq"] = f(nq)
    w["w_ksd"] = f(np.concatenate([ks[:, 0:64], ks[:, 0:64], ks[:, 64:128], ks[:, 64:128]], axis=1))
    w["w_kwd"] = f(np.concatenate([kw[:, 0:64], kw[:, 0:64], kw[:, 64:128], kw[:, 64:128]], axis=1))
    kvc = np.concatenate([kc, vc], axis=1)
    w["w_kvc2"] = f(np.concatenate([np.concatenate([kvc[:, i * 64:(i + 1) * 64]] * 2, axis=1) for i in range(4)], axis=1))
    w["w_vtok"] = f(np.concatenate([vs[:, 0:64], vw[:, 0:64], vs[:, 64:128], vw[:, 64:128], ng], axis=1))
    w["conv_wT"] = f(np.asarray(inp["conv_w"][0]).T)
    w["ml_b_i"] = f(np.asarray(inp["ml_b_i"][0])[:, None])
    w["ml_b_f"] = f(np.asarray(inp["ml_b_f"][0])[:, None])
    w["ml_gn"] = f(np.asarray(inp["ml_gn"][0])[None, :])
    for kv in ("k", "v"):
        w[f"cmp_{kv}_peT2"] = f(np.asarray(inp[f"cmp_{kv}_pe"][0]).reshape(16, 128).T)
        w[f"cmp_{kv}_w1"] = f(inp[f"cmp_{kv}_w1"][0])
        w[f"cmp_{kv}_b1"] = f(np.asarray(inp[f"cmp_{kv}_b1"][0]).reshape(2, 128).T)
    kw2 = np.asarray(inp["cmp_k_w2"][0], dtype=np.float32)
    w["cmp_k_w2d"] = f(np.concatenate([kw2, kw2], axis=1))
    w["cmp_v_w2"] = f(inp["cmp_v_w2"][0])
    w["w_out"] = f(inp["w_out"][0])
    w.update(_consts())
    return w


def kernel(**inputs):
    n_cores = 8
    x = np.asarray(inputs["x"], dtype=np.float32)
    nseq = x.shape[0] // n_cores
    w = _prep_weights(inputs)
    nc = build_program(nseq)
    in_maps = []
    for c in range(n_cores):
        m = dict(w)
        m["x"] = np.ascontiguousarray(x[c * nseq:(c + 1) * nseq])
        in_maps.append(m)
    res = run_bass_kernel_spmd(nc, in_maps, core_ids=list(range(n_cores)))
    return np.concatenate([r["out"] for r in res.results], axis=0).astype(np.float32)
```
